# Optimizing a Trainium2 kernel written in Bass

```python
import math
import jax, jax.numpy as jnp
from jax import lax
import numpy as np

D_MODEL = 1024
BATCH = 4
SEQ = 4096
DEPTH = 4

N_MIXERS = 2
D_FF = 2816
CONV_WIDTH = 31
N_HEADS = 8
HEAD_DIM = 64
QK_WIDTH = N_HEADS * 2 * HEAD_DIM
V_WIDTH = N_HEADS * 2 * HEAD_DIM
Q_BLOCK = 128
RMS_EPS = 1e-6
SUBLN_EPS = 1e-5
LN_EPS = 1e-5
N_CONV_LAYERS = (DEPTH + 1) // 2
N_ATTN_LAYERS = DEPTH // 2

kernel_name = "hybrid_conformer_diffattn_macaron_encoder"


def _rmsnorm(x, g, eps=RMS_EPS):
    xf = x.astype(jnp.float32)
    y = xf * lax.rsqrt(jnp.mean(xf * xf, axis=-1, keepdims=True) + eps)
    return (y * g.astype(jnp.float32)).astype(x.dtype)


def _layernorm(x, g, b, eps=LN_EPS):
    xf = x.astype(jnp.float32)
    mu = jnp.mean(xf, axis=-1, keepdims=True)
    var = jnp.mean(jnp.square(xf - mu), axis=-1, keepdims=True)
    y = (xf - mu) * lax.rsqrt(var + eps)
    return (y * g.astype(jnp.float32) + b.astype(jnp.float32)).astype(x.dtype)


def _half_ffn(x, g_pre, g_post, w_gate, w_up, w_down):
    h = _rmsnorm(x, g_pre)
    f = (jax.nn.silu(h @ w_gate) * (h @ w_up)) @ w_down
    return x + 0.5 * _rmsnorm(f, g_post)


def _conformer_conv(h, w_pw1, b_pw1, w_dw, b_dw, ln_g, ln_b, w_pw2, b_pw2):
    u = h @ w_pw1 + b_pw1
    u = u[..., :D_MODEL] * jax.nn.sigmoid(u[..., D_MODEL:])
    pad = (CONV_WIDTH - 1) // 2
    c = lax.conv_general_dilated(
        u, w_dw[:, None, :].astype(u.dtype),
        window_strides=(1,), padding=[(pad, pad)],
        dimension_numbers=("NWC", "WIO", "NWC"),
        feature_group_count=D_MODEL) + b_dw
    c = jax.nn.silu(_layernorm(c, ln_g, ln_b))
    return c @ w_pw2 + b_pw2


def _lambda_init(layer_idx):
    return 0.8 - 0.6 * math.exp(-0.3 * layer_idx)


def _diff_attention(h, w_qkv, w_o, lq1, lk1, lq2, lk2, subln_g, lambda_init):
    B, S, _ = h.shape
    qkv = h @ w_qkv
    q, k, v = jnp.split(qkv, [QK_WIDTH, 2 * QK_WIDTH], axis=-1)
    q = q.reshape(B, S, N_HEADS, 2, HEAD_DIM)
    k = k.reshape(B, S, N_HEADS, 2, HEAD_DIM)
    v = v.reshape(B, S, N_HEADS, 2 * HEAD_DIM)
    f32 = jnp.float32
    lam = (jnp.exp(jnp.sum(lq1.astype(f32) * lk1.astype(f32)))
           - jnp.exp(jnp.sum(lq2.astype(f32) * lk2.astype(f32))) + lambda_init)
    slopes = 2.0 ** (-8.0 * (jnp.arange(N_HEADS, dtype=f32) + 1.0) / N_HEADS)
    scale = HEAD_DIM ** -0.5
    kpos = jnp.arange(S, dtype=f32)
    nblk = S // Q_BLOCK
    q_blocks = q.reshape(B, nblk, Q_BLOCK, N_HEADS, 2, HEAD_DIM).transpose(1, 0, 2, 3, 4, 5)
    starts = jnp.arange(nblk, dtype=jnp.int32) * Q_BLOCK

    def one_block(args):
        qb, start = args
        s = jnp.einsum("bqhmd,bkhmd->bhmqk", qb, k).astype(f32) * scale
        qpos = start.astype(f32) + jnp.arange(Q_BLOCK, dtype=f32)
        dist = jnp.abs(qpos[:, None] - kpos[None, :])
        s = s - slopes[:, None, None, None] * dist
        p = jax.nn.softmax(s, axis=-1)
        a = p[:, :, 0] - lam * p[:, :, 1]
        return jnp.einsum("bhqk,bkhe->bqhe", a.astype(v.dtype), v)

    o = lax.map(one_block, (q_blocks, starts))
    o = o.transpose(1, 0, 2, 3, 4).reshape(B, S, N_HEADS, 2 * HEAD_DIM)
    o = _rmsnorm(o, subln_g, SUBLN_EPS) * (1.0 - lambda_init)
    return o.reshape(B, S, V_WIDTH) @ w_o


def setup_inputs(seed: int = 0) -> dict:
    key = jax.random.key(seed)
    ks = jax.random.split(key, 24)
    D, F, K = D_MODEL, D_FF, CONV_WIDTH
    NC, NA = N_CONV_LAYERS, N_ATTN_LAYERS
    nrm = jax.random.normal
    f32 = jnp.float32

    def gain(k, shape):
        return 1.0 + 0.05 * nrm(k, shape, f32)

    def small(k, shape):
        return 0.02 * nrm(k, shape, f32)

    return {
        "x": nrm(ks[0], (BATCH, SEQ, D), f32),
        "ffn_norm_pre": gain(ks[1], (DEPTH, 2, D)),
        "ffn_norm_post": gain(ks[2], (DEPTH, 2, D)),
        "ffn_w_gate": nrm(ks[3], (DEPTH, 2, D, F), f32) * D ** -0.5,
        "ffn_w_up": nrm(ks[4], (DEPTH, 2, D, F), f32) * D ** -0.5,
        "ffn_w_down": nrm(ks[5], (DEPTH, 2, F, D), f32) * F ** -0.5,
        "mix_norm_pre": gain(ks[6], (DEPTH, D)),
        "mix_norm_post": gain(ks[7], (DEPTH, D)),
        "conv_w_pw1": nrm(ks[8], (NC, D, 2 * D), f32) * D ** -0.5,
        "conv_b_pw1": small(ks[9], (NC, 2 * D)),
        "conv_w_dw": nrm(ks[10], (NC, K, D), f32) * K ** -0.5,
        "conv_b_dw": small(ks[11], (NC, D)),
        "conv_ln_g": gain(ks[12], (NC, D)),
        "conv_ln_b": small(ks[13], (NC, D)),
        "conv_w_pw2": nrm(ks[14], (NC, D, D), f32) * D ** -0.5,
        "conv_b_pw2": small(ks[15], (NC, D)),
        "attn_w_qkv": nrm(ks[16], (NA, D, 2 * QK_WIDTH + V_WIDTH), f32) * D ** -0.5,
        "attn_w_o": nrm(ks[17], (NA, V_WIDTH, D), f32) * V_WIDTH ** -0.5,
        "attn_lambda_q1": 0.1 * nrm(ks[18], (NA, HEAD_DIM), f32),
        "attn_lambda_k1": 0.1 * nrm(ks[19], (NA, HEAD_DIM), f32),
        "attn_lambda_q2": 0.1 * nrm(ks[20], (NA, HEAD_DIM), f32),
        "attn_lambda_k2": 0.1 * nrm(ks[21], (NA, HEAD_DIM), f32),
        "attn_subln_g": gain(ks[22], (NA, 2 * HEAD_DIM)),
    }


def reference(x, ffn_norm_pre, ffn_norm_post, ffn_w_gate, ffn_w_up, ffn_w_down,
              mix_norm_pre, mix_norm_post,
              conv_w_pw1, conv_b_pw1, conv_w_dw, conv_b_dw, conv_ln_g, conv_ln_b,
              conv_w_pw2, conv_b_pw2,
              attn_w_qkv, attn_w_o, attn_lambda_q1, attn_lambda_k1,
              attn_lambda_q2, attn_lambda_k2, attn_subln_g):
    ic = 0
    ia = 0
    for i in range(DEPTH):
        x = _half_ffn(x, ffn_norm_pre[i, 0], ffn_norm_post[i, 0],
                      ffn_w_gate[i, 0], ffn_w_up[i, 0], ffn_w_down[i, 0])
        h = _rmsnorm(x, mix_norm_pre[i])
        if i % N_MIXERS == 0:
            m = _conformer_conv(h, conv_w_pw1[ic], conv_b_pw1[ic], conv_w_dw[ic],
                                conv_b_dw[ic], conv_ln_g[ic], conv_ln_b[ic],
                                conv_w_pw2[ic], conv_b_pw2[ic])
            ic += 1
        else:
            m = _diff_attention(h, attn_w_qkv[ia], attn_w_o[ia],
                                attn_lambda_q1[ia], attn_lambda_k1[ia],
                                attn_lambda_q2[ia], attn_lambda_k2[ia],
                                attn_subln_g[ia], _lambda_init(i))
            ia += 1
        x = x + _rmsnorm(m, mix_norm_post[i])
        x = _half_ffn(x, ffn_norm_pre[i, 1], ffn_norm_post[i, 1],
                      ffn_w_gate[i, 1], ffn_w_up[i, 1], ffn_w_down[i, 1])
    return x
```

```python
import math
from contextlib import ExitStack
import numpy as np
import concourse.bass as bass
import concourse.mybir as mybir
from concourse.bass_utils import run_bass_kernel_spmd

F32, BF16, I16 = mybir.dt.float32, mybir.dt.bfloat16, mybir.dt.int16
AF = mybir.ActivationFunctionType
ALU = mybir.AluOpType
RMS_EPS, SUBLN_EPS, LN_EPS = 1e-6, 1e-5, 1e-5
CONVW = 31
NDS = 24


class Cfg:
    def __init__(self, D=1024, F=2816, T=2048, TT=512, L=4, ncores=8):
        self.D, self.F, self.T, self.TT, self.L, self.ncores = D, F, T, TT, L, ncores
        self.DC, self.FC, self.NT, self.S, self.H = D // 128, F // 128, T // TT, 2 * T, D // 128
        self.TH = T // 2
        self.NTH = self.TH // TT
        self.GW = 256
        self.NDCL = min(4, self.DC)
        self.PW = self.NDCL * 128
        self.NTAB = 3 * T - 128
        assert self.NDCL * self.NTH <= 8 and self.NTH * TT == self.TH


def param_layout(cfg):
    DC, L = cfg.DC, cfg.L
    off, n = {}, 0

    def add(name, w):
        nonlocal n
        off[name] = n
        n += w

    add("ident", 128)
    add("mL", 1)
    add("mR", 1)
    for i in range(L):
        for j in range(2):
            add(f"fpre{i}{j}", DC)
            add(f"fpost{i}{j}", DC)
        add(f"mpre{i}", DC)
        add(f"mpost{i}", DC)
        if i % 2 == 0:
            add(f"bpw1{i}", 2 * DC)
            add(f"wdw{i}", DC * CONVW)
            for nm in ("bdw", "lng", "lnb", "bpw2"):
                add(f"{nm}{i}", DC)
        else:
            for nm in ("subg", "lq1", "lk1", "lq2", "lk2"):
                add(f"{nm}{i}", 1)
    return off, n


class Prog:
    def __init__(self):
        self.names = ["pe", "act", "dve", "pool", "sp"]
        self.ops = {k: [] for k in self.names}
        self.cnt = {k: 0 for k in ("pe", "act", "dve")}
        self.seen = {k: {} for k in self.names}
        self.lastw, self.readers = {}, {}
        self.dma_tot = [0] * NDS
        self.dma_next = {"sp": 0, "pool": 0}
        self.ncc = 0

    def _deps(self, eng, reads, writes, extra=()):
        deps = {}

        def add(tok):
            if tok is not None:
                deps[tok[0]] = max(deps.get(tok[0], 0), tok[1])

        for r in reads:
            add(self.lastw.get(r))
        for w in writes:
            add(self.lastw.get(w))
            for kv in self.readers.get(w, {}).items():
                add(kv)
        for t in extra:
            add(t)
        waits = []
        for k, v in deps.items():
            if k == eng and eng == "pe":
                continue
            if self.seen[eng].get(k, 0) >= v:
                continue
            self.seen[eng][k] = v
            waits.append((k, v))
        return waits

    def _record(self, tok, reads, writes):
        for r in reads:
            d = self.readers.setdefault(r, {})
            d[tok[0]] = max(d.get(tok[0], 0), tok[1])
        for w in writes:
            self.lastw[w] = tok
            self.readers[w] = {}

    def op(self, eng, fn, reads=(), writes=(), signal=True):
        waits = self._deps(eng, reads, writes)
        if signal:
            self.cnt[eng] += 1
            tok = (eng, self.cnt[eng])
        else:
            tok = (eng, self.cnt[eng] + 1)
        self.ops[eng].append((waits, fn, eng if signal else None, 1))
        self._record(tok, reads, writes)

    def dma(self, q, fn, reads=(), writes=()):
        half = NDS // 2
        j = self.dma_next[q]
        self.dma_next[q] = (j + 1) % half
        i = j + (half if q == "pool" else 0)
        key = ("d", i)
        extra = [(key, self.dma_tot[i])] if self.dma_tot[i] else []
        waits = self._deps(q, reads, writes, extra)
        self.dma_tot[i] += 16
        self.ops[q].append((waits, fn, key, 16))
        self._record((key, self.dma_tot[i]), reads, writes)

    def collective(self, fn, reads=(), writes=()):
        key = ("c", self.ncc)
        self.ncc += 1
        waits = self._deps("pool", reads, writes)
        self.ops["pool"].append((waits, fn, key, None))
        self._record((key, 1), reads, writes)

    def barrier(self):
        cur = [(k, v) for k, v in self.cnt.items() if v]
        cur += [(("d", i), t) for i, t in enumerate(self.dma_tot) if t]
        cur += [(("c", j), 1) for j in range(self.ncc)]
        for e in self.names:
            waits = []
            for k, v in cur:
                if k == e and e == "pe":
                    continue
                if self.seen[e].get(k, 0) >= v:
                    continue
                self.seen[e][k] = v
                waits.append((k, v))
            if waits:
                self.ops[e].append((waits, None, None, 0))

    def replay(self, name, e, sems):
        for waits, fn, inckey, amt in self.ops[name]:
            for k, v in waits:
                e.wait_ge(sems[k], v)
            if fn is None:
                continue
            ins = fn(e)
            if inckey is not None:
                if amt is None:
                    ins.then_inc(sems[inckey])
                else:
                    ins.then_inc(sems[inckey], amt)


class Ring:
    def __init__(self, n):
        self.n, self.i = n, 0

    def next(self):
        v = self.i
        self.i = (self.i + 1) % self.n
        return v


def lambda_init(i):
    return 0.8 - 0.6 * math.exp(-0.3 * i)


def build(cfg):
    D, F, T, TT, L = cfg.D, cfg.F, cfg.T, cfg.TT, cfg.L
    DC, FC, NT, S, H, TH, NTH, GW, NDCL, PW = cfg.DC, cfg.FC, cfg.NT, cfg.S, cfg.H, cfg.TH, cfg.NTH, cfg.GW, cfg.NDCL, cfg.PW
    NTAB = cfg.NTAB
    NC2, NA2 = (L + 1) // 2, L // 2
    poff, NP = param_layout(cfg)
    nc = bass.Bass("TRN2", target_bir_lowering=False)
    dt = nc.dram_tensor
    xT_d = dt("xT", [D, T], F32, kind="ExternalInput").ap()
    par_d = dt("par", [128, NP], F32, kind="ExternalInput").ap()
    tab_d = dt("tab", [128, NTAB], I16, kind="ExternalInput").ap()
    wg_d = dt("wg", [L, 2, D, F], F32, kind="ExternalInput").ap()
    wu_d = dt("wu", [L, 2, D, F], F32, kind="ExternalInput").ap()
    wd_d = dt("wd", [L, 2, F, D], F32, kind="ExternalInput").ap()
    pw1_d = dt("pw1", [NC2, D, 2 * D], F32, kind="ExternalInput").ap()
    pw2_d = dt("pw2", [NC2, D, D], F32, kind="ExternalInput").ap()
    qkv_d = dt("qkv", [max(NA2, 1), D, 3 * D], F32, kind="ExternalInput").ap()
    wo_d = dt("wo", [max(NA2, 1), D, D], F32, kind="ExternalInput").ap()
    yT_d = dt("yT", [D, T], F32, kind="ExternalOutput").ap()
    q_loc = dt("q_loc", [D, T], BF16).ap()
    kt_loc = dt("kt_loc", [D, T], BF16).ap()
    kt_all = dt("kt_all", [2 * D, T], BF16).ap()
    v_loc = dt("v_loc", [T, D], BF16).ap()
    v_all = dt("v_all", [2 * T, D], BF16).ap()
    halo_loc = dt("halo_loc", [D, 32], BF16).ap()
    halo_all = dt("halo_all", [2 * D, 32], BF16).ap()
    groups = [[2 * i, 2 * i + 1] for i in range(cfg.ncores // 2)]
    CRV = getattr(cfg, 'CRV', None) or min(T, max(128, 262144 // D))
    NJ = T // CRV

    P = Prog()
    HBN = DC * T
    SCRN = max(DC * (T + 32), FC * TH - HBN // 2, NTAB + S + S + 2 * T) + 16
    FTN = max(DC * TH, DC * TT + CONVW * 128, 12 * TT + T)
    WSL = max(2 * DC * max(GW, min(512, D) // 2), CONVW * 128)

    with ExitStack() as es:
        sb = lambda name, shape, d: es.enter_context(nc.sbuf_tensor(name, shape, d))
        X = sb("X", [128, DC * T], F32)
        HS = sb("HS", [128, HBN + SCRN], BF16)
        FT = sb("FT", [128, FTN], F32)
        WR = sb("WR", [128, 2 * WSL], BF16)
        WDR = sb("WDR", [128, 4 * PW], BF16)
        TMP = sb("TMP", [128, 6 * TT], F32)
        SQ = sb("SQ", [128, 2 * TT], BF16)
        PAR = sb("PAR", [128, NP], F32)
        DER = sb("DER", [128, 16 * L + 8], F32)
        ONESB = sb("ONESB", [128, 128], BF16)
        ONESF = sb("ONESF", [128, 128], F32)
        GH = sb("GH", [128, L * 2 * DC], F32)
        EPST = sb("EPST", [128, 4], F32)
        pb = [es.enter_context(nc.psum_tensor(f"pb{i}", [128, 512], F32)) for i in range(8)]
        sems = {}
        for k in ("pe", "act", "dve"):
            sems[k] = es.enter_context(nc.semaphore(f"s_{k}"))
        for i in range(NDS):
            sems[("d", i)] = es.enter_context(nc.semaphore(f"s_d{i}"))
        for j in range((L // 2) * (H + T // CRV) + (L + 1) // 2 + 1):
            sems[("c", j)] = es.enter_context(nc.semaphore(f"s_c{j}"))
        block = es.enter_context(nc.Block())

        Xv = X[:, :].rearrange("p (c t) -> p c t", c=DC)
        HB = HS[:, 0:HBN].rearrange("p (c t) -> p c t", c=DC)
        tmp_ring, sq_ring, wr_ring, wdr_ring = Ring(6), Ring(2), Ring(2), Ring(4)
        pc = lambda name, k=0: PAR[:, poff[name] + k: poff[name] + k + 1]
        tmpv = lambda s: TMP[:, s * TT:(s + 1) * TT]
        sqv = lambda s: SQ[:, s * TT:(s + 1) * TT]

        P.dma("sp", lambda e: e.dma_start(out=PAR[:, :], in_=par_d[:, :]), writes=["par"])
        for c in range(DC):
            P.dma("sp", lambda e, c=c: e.dma_start(out=Xv[:, c, :], in_=xT_d[c * 128:(c + 1) * 128, :]),
                  writes=[("x", c, t) for t in range(NT)])
        P.op("dve", lambda e: e.memset(ONESB[:, :], 1.0), writes=["onesb"])
        P.op("dve", lambda e: e.memset(ONESF[:, :], 1.0), writes=["onesf"])
        for i in range(L):
            for j in range(2):
                o = (i * 2 + j) * DC
                P.op("dve", lambda e, o=o, i=i, j=j: e.tensor_scalar(
                    out=GH[:, o:o + DC], in0=PAR[:, poff[f"fpost{i}{j}"]:poff[f"fpost{i}{j}"] + DC],
                    scalar1=0.5, scalar2=None, op0=ALU.mult), reads=["par"], writes=["gh"])

        def rms_rstd(src_fn, n, eps, inv_n, bank, use_ln=False):
            for c in range(n):
                ap, res = src_fn(c)
                s = sq_ring.next()
                P.op("act", lambda e, ap=ap, s=s: e.activation(out=sqv(s), in_=ap, func=AF.Square),
                     reads=[res], writes=[("sq", s)])
                P.op("pe", lambda e, s=s, c=c: e.matmul(pb[bank][:, 0:TT], lhsT=ONESB[:, :], rhs=sqv(s),
                                                       start=(c == 0), stop=(c == n - 1)),
                     reads=[("sq", s), "onesb"], writes=[("pb", bank)], signal=True)
            r = tmp_ring.next()
            if use_ln:
                P.op("act", lambda e, r=r: e.activation(out=tmpv(r), in_=pb[bank][:, 0:TT], func=AF.Ln,
                                                       bias=EPSC[eps], scale=inv_n),
                     reads=[("pb", bank), "epsc"], writes=[("tmp", r)])
                P.op("act", lambda e, r=r: e.activation(out=tmpv(r), in_=tmpv(r), func=AF.Exp, scale=-0.5),
                     reads=[("tmp", r)], writes=[("tmp", r)])
                return r
            P.op("act", lambda e, r=r: e.activation(out=tmpv(r), in_=pb[bank][:, 0:TT], func=AF.Sqrt,
                                                   bias=EPSC[eps], scale=inv_n),
                 reads=[("pb", bank), "epsc"], writes=[("tmp", r)])
            P.op("dve", lambda e, r=r: e.reciprocal(out=tmpv(r), in_=tmpv(r)), reads=[("tmp", r)], writes=[("tmp", r)])
            return r

        EPSC = {RMS_EPS: EPST[:, 0:1], SUBLN_EPS: EPST[:, 1:2], 0.0: EPST[:, 2:3]}
        P.op("dve", lambda e: e.memset(EPST[:, 0:1], RMS_EPS), writes=["epsc"])
        P.op("dve", lambda e: e.memset(EPST[:, 1:2], SUBLN_EPS), writes=["epsc"])
        P.op("dve", lambda e: e.memset(EPST[:, 2:3], 0.0), writes=["epsc"])

        def prenorm(gname, tiles, hdst):
            for n_, tl in enumerate(tiles):
                tok = tl * TT
                r = rms_rstd(lambda c, tok=tok, tl=tl: (Xv[:, c, tok:tok + TT], ("x", c, tl)), DC, RMS_EPS, 1.0 / D, n_ % 8)
                for c in range(DC):
                    P.op("dve", lambda e, c=c, tok=tok, r=r, n_=n_: e.scalar_tensor_tensor(
                        out=hdst(c, n_), in0=Xv[:, c, tok:tok + TT], scalar=pc(gname, c), in1=tmpv(r),
                        op0=ALU.mult, op1=ALU.mult),
                        reads=[("x", c, tl), ("tmp", r), "par"], writes=[("h", c, n_)])

        def wview(slot, part, w):
            base = slot * WSL + part * (WSL // 2)
            return WR[:, base:base + DC * w].rearrange("p (k f) -> p k f", k=DC)

        def load_w(slot, part, w2d, c0, w):
            src = w2d.rearrange("(k p) f -> p k f", p=128)[:, :, c0:c0 + w]
            P.dma("pool", lambda e: e.dma_start(out=wview(slot, part, w), in_=src), writes=[("wr", slot, part)])

        def gated_proj(hv, w_act, off_act, w_lin, off_lin, ncols, ntiles, func, b_act, b_lin, out_fn):
            pr = Ring(4)
            for c0 in range(0, ncols, GW):
                w = min(GW, ncols - c0)
                slot = wr_ring.next()
                load_w(slot, 0, w_act, off_act + c0, w)
                load_w(slot, 1, w_lin, off_lin + c0, w)
                for cl in range(w // 128):
                    c = c0 // 128 + cl
                    for tt in range(ntiles):
                        p2 = pr.next()
                        ba, bl = 2 * p2, 2 * p2 + 1
                        for part, bk in ((0, ba), (1, bl)):
                            for kc in range(DC):
                                P.op("pe", lambda e, part=part, bk=bk, kc=kc, cl=cl, tt=tt, slot=slot, w=w: e.matmul(
                                    pb[bk][:, 0:TT], lhsT=wview(slot, part, w)[:, kc, cl * 128:(cl + 1) * 128],
                                    rhs=hv[:, kc, tt * TT:(tt + 1) * TT], start=(kc == 0), stop=(kc == DC - 1)),
                                    reads=[("wr", slot, part), ("h", kc, tt)], writes=[("pb", bk)], signal=(kc == DC - 1))
                        s = tmp_ring.next()
                        if b_act is None:
                            P.op("act", lambda e, ba=ba, s=s: e.activation(out=tmpv(s), in_=pb[ba][:, 0:TT], func=func),
                                 reads=[("pb", ba)], writes=[("tmp", s)])
                        else:
                            P.op("act", lambda e, ba=ba, s=s, c=c: e.activation(out=tmpv(s), in_=pb[ba][:, 0:TT], func=func,
                                                                               bias=b_act(c)),
                                 reads=[("pb", ba), "par"], writes=[("tmp", s)])
                        oap, ores = out_fn(c, tt)
                        if b_lin is None:
                            P.op("dve", lambda e, bl=bl, s=s, oap=oap: e.tensor_tensor(out=oap, in0=pb[bl][:, 0:TT], in1=tmpv(s), op=ALU.mult),
                                 reads=[("pb", bl), ("tmp", s)], writes=[ores])
                        else:
                            P.op("dve", lambda e, bl=bl, s=s, oap=oap, c=c: e.scalar_tensor_tensor(
                                out=oap, in0=pb[bl][:, 0:TT], scalar=b_lin(c), in1=tmpv(s), op0=ALU.add, op1=ALU.mult),
                                reads=[("pb", bl), ("tmp", s), "par"], writes=[ores])

        FTv = FT[:, 0:DC * TH].rearrange("p (c t) -> p c t", c=DC)

        def proj_post(in_fn, KC, w2d, bias, gain_fn, hf, do=("mm", "post")):
            for ps in range(D // PW if "mm" in do else 0):
                for kc in range(KC):
                    slot = wdr_ring.next()
                    src = w2d[kc * 128:(kc + 1) * 128, ps * PW:(ps + 1) * PW]
                    P.dma("pool", lambda e, slot=slot, src=src: e.dma_start(out=WDR[:, slot * PW:(slot + 1) * PW], in_=src),
                          writes=[("wdr", slot)])
                    for dl in range(NDCL):
                        for tt in range(NTH):
                            iap, ires = in_fn(kc, tt)
                            bk = dl * NTH + tt
                            P.op("pe", lambda e, slot=slot, dl=dl, iap=iap, bk=bk, kc=kc: e.matmul(
                                pb[bk][:, 0:TT], lhsT=WDR[:, slot * PW + dl * 128: slot * PW + (dl + 1) * 128], rhs=iap,
                                start=(kc == 0), stop=(kc == KC - 1)),
                                reads=[("wdr", slot), ires], writes=[("pb", bk)],
                                signal=(kc == KC - 1 or (dl == NDCL - 1 and tt == NTH - 1)))
                for dl in range(NDCL):
                    dc = ps * NDCL + dl
                    for tt in range(NTH):
                        bk = dl * NTH + tt
                        dst = FTv[:, dc, tt * TT:(tt + 1) * TT]
                        if bias is not None:
                            P.op("act", lambda e, bk=bk, dst=dst, dc=dc: e.activation(out=dst, in_=pb[bk][:, 0:TT], func=AF.Identity,
                                                                                   bias=pc(bias, dc)),
                                 reads=[("pb", bk), "par"], writes=[("ft", dc, tt)])
                        elif (dl + tt) % 2 == 0:
                            P.op("act", lambda e, bk=bk, dst=dst: e.activation(out=dst, in_=pb[bk][:, 0:TT], func=AF.Copy),
                                 reads=[("pb", bk)], writes=[("ft", dc, tt)])
                        else:
                            P.op("dve", lambda e, bk=bk, dst=dst: e.tensor_copy(out=dst, in_=pb[bk][:, 0:TT]),
                                 reads=[("pb", bk)], writes=[("ft", dc, tt)])
            for tt in range(NTH if "post" in do else 0):
                tl = hf * NTH + tt
                tok = tl * TT
                r = rms_rstd(lambda c, tt=tt: (FTv[:, c, tt * TT:(tt + 1) * TT], ("ft", c, tt)), DC, RMS_EPS, 1.0 / D, tt)
                for c in range(DC):
                    f_ap = FTv[:, c, tt * TT:(tt + 1) * TT]
                    P.op("dve", lambda e, f_ap=f_ap, c=c, r=r: e.scalar_tensor_tensor(
                        out=f_ap, in0=f_ap, scalar=gain_fn(c), in1=tmpv(r), op0=ALU.mult, op1=ALU.mult),
                        reads=[("ft", c, tt), ("tmp", r), "par", "gh"], writes=[("ft", c, tt)])
                    P.op("dve", lambda e, f_ap=f_ap, c=c, tok=tok: e.tensor_tensor(
                        out=Xv[:, c, tok:tok + TT], in0=Xv[:, c, tok:tok + TT], in1=f_ap, op=ALU.add),
                        reads=[("ft", c, tt), ("x", c, tl)], writes=[("x", c, tl)])

        A_all = HS[:, HBN // 2: HBN // 2 + FC * TH].rearrange("p (c t) -> p c t", c=FC)
        HBh = HS[:, 0:DC * TH].rearrange("p (c t) -> p c t", c=DC)

        def ffn(i, j):
            go = (i * 2 + j) * DC
            pre = lambda hf: prenorm(f"fpre{i}{j}", [hf * NTH + tt for tt in range(NTH)], lambda c, n_: HBh[:, c, n_ * TT:(n_ + 1) * TT])
            gu = lambda: gated_proj(HBh, wg_d[i, j], 0, wu_d[i, j], 0, F, NTH, AF.Silu, None, None,
                                    lambda c, tt: (A_all[:, c, tt * TT:(tt + 1) * TT], ("A", c, tt)))
            down = lambda hf, do: proj_post(lambda kc, tt: (A_all[:, kc, tt * TT:(tt + 1) * TT], ("A", kc, tt)), FC, wd_d[i, j], None,
                                            lambda c: GH[:, go + c:go + c + 1], hf, do)
            P.barrier()
            pre(0)
            gu()
            pre(1)
            down(0, ("mm",))
            gu()
            down(0, ("post",))
            down(1, ("mm",))
            down(1, ("post",))

        SCR0 = HBN
        Uv = HS[:, SCR0:SCR0 + DC * (T + 32)].rearrange("p (c t) -> p c t", c=DC)
        CT = FT[:, 0:DC * TT].rearrange("p (c t) -> p c t", c=DC)

        def conv(i, ic):
            P.barrier()
            prenorm(f"mpre{i}", list(range(NT)), lambda c, n_: HB[:, c, n_ * TT:(n_ + 1) * TT])
            gated_proj(HB, pw1_d[ic], D, pw1_d[ic], 0, D, NT, AF.Sigmoid, lambda c: pc(f"bpw1{i}", DC + c), lambda c: pc(f"bpw1{i}", c),
                       lambda c, tt: (Uv[:, c, 16 + tt * TT:16 + (tt + 1) * TT], ("u", c, tt)))
            ures = [("u", c, t) for c in range(DC) for t in range(NT)]
            hl = halo_loc.rearrange("(c p) t -> p c t", p=128)
            P.dma("sp", lambda e: e.dma_start(out=hl[:, :, 0:16], in_=Uv[:, :, 16:32]), reads=ures, writes=["halo_loc"])
            P.dma("sp", lambda e: e.dma_start(out=hl[:, :, 16:32], in_=Uv[:, :, T:T + 16]), reads=ures, writes=["halo_loc"])
            P.collective(lambda e: e.collective_compute("AllGather", ALU.bypass, replica_groups=groups,
                                                        ins=[halo_loc.opt()], outs=[halo_all.opt()]),
                         reads=["halo_loc"], writes=["halo_all"])
            ha = halo_all.rearrange("(r c p) t -> r p c t", p=128, c=DC)
            P.dma("sp", lambda e: e.dma_start(out=Uv[:, :, 0:16], in_=ha[0][:, :, 16:32]), reads=["halo_all"], writes=["uL"])
            P.dma("sp", lambda e: e.dma_start(out=Uv[:, :, T + 16:T + 32], in_=ha[1][:, :, 0:16]), reads=["halo_all"], writes=["uR"])
            P.op("dve", lambda e: e.tensor_scalar(out=Uv[:, :, 0:16], in0=Uv[:, :, 0:16], scalar1=pc("mL"), scalar2=None, op0=ALU.mult),
                 reads=["uL", "par"], writes=["uL"])
            P.op("dve", lambda e: e.tensor_scalar(out=Uv[:, :, T + 16:T + 32], in0=Uv[:, :, T + 16:T + 32], scalar1=pc("mR"), scalar2=None,
                                                  op0=ALU.mult), reads=["uR", "par"], writes=["uR"])
            P.barrier()
            DGs = [WR[:, sl * WSL:sl * WSL + CONVW * 128].rearrange("p (j m) -> p j m", j=CONVW) for sl in range(2)]
            dg_ring = Ring(2)
            CBv = lambda s: SQ[:, s * TT:(s + 1) * TT]
            for tt in range(NT):
                for c in range(DC):
                    dsl = dg_ring.next()
                    DGv = DGs[dsl]
                    for j in range(CONVW):
                        P.op("dve", lambda e, c=c, j=j, DGv=DGv: e.tensor_scalar(
                            out=DGv[:, j, :], in0=PAR[:, poff["ident"]:poff["ident"] + 128],
                            scalar1=pc(f"wdw{i}", c * CONVW + j), scalar2=None, op0=ALU.mult),
                            reads=["par"], writes=[("dg", dsl)])
                    bk = c % 4
                    for j in range(CONVW):
                        P.op("pe", lambda e, c=c, j=j, bk=bk, tt=tt, DGv=DGv: e.matmul(
                            pb[bk][:, 0:TT], lhsT=DGv[:, j, :], rhs=Uv[:, c, tt * TT + j + 1: tt * TT + j + 1 + TT],
                            start=(j == 0), stop=(j == CONVW - 1)),
                            reads=[("dg", dsl), "uL", "uR"] + [("u", c, t) for t in range(max(0, tt - 1), min(NT, tt + 2))],
                            writes=[("pb", bk)], signal=(j == CONVW - 1))
                    P.op("act", lambda e, c=c, bk=bk: e.activation(out=CT[:, c, :], in_=pb[bk][:, 0:TT], func=AF.Identity,
                                                                    bias=pc(f"bdw{i}", c)),
                         reads=[("pb", bk), "par"], writes=[("ct", c)])
                    s = sq_ring.next()
                    P.op("act", lambda e, c=c, s=s: e.activation(out=sqv(s), in_=CT[:, c, :], func=AF.Square),
                         reads=[("ct", c)], writes=[("sq", s)])
                    P.op("pe", lambda e, c=c, s=s: e.matmul(pb[4][:, 0:TT], lhsT=ONESB[:, :], rhs=sqv(s), start=(c == 0), stop=(c == DC - 1)),
                         reads=[("sq", s), "onesb"], writes=[("pb", 4)], signal=True)
                    s2 = sq_ring.next()
                    P.op("act", lambda e, c=c, s2=s2: e.activation(out=sqv(s2), in_=CT[:, c, :], func=AF.Copy), reads=[("ct", c)], writes=[("sq", s2)])
                    P.op("pe", lambda e, c=c, s2=s2: e.matmul(pb[5][:, 0:TT], lhsT=ONESB[:, :], rhs=sqv(s2), start=(c == 0), stop=(c == DC - 1)),
                         reads=[("sq", s2), "onesb"], writes=[("pb", 5)], signal=True)
                m1, m2, m3 = tmp_ring.next(), tmp_ring.next(), tmp_ring.next()
                P.op("dve", lambda e, m1=m1: e.tensor_scalar(out=tmpv(m1), in0=pb[5][:, 0:TT], scalar1=1.0 / D, scalar2=None, op0=ALU.mult),
                     reads=[("pb", 5)], writes=[("tmp", m1)])
                P.op("dve", lambda e, m1=m1, m2=m2: e.tensor_tensor(out=tmpv(m2), in0=tmpv(m1), in1=tmpv(m1), op=ALU.mult),
                     reads=[("tmp", m1)], writes=[("tmp", m2)])
                P.op("dve", lambda e, m2=m2, m3=m3: e.scalar_tensor_tensor(out=tmpv(m3), in0=pb[4][:, 0:TT], scalar=1.0 / D, in1=tmpv(m2),
                                                                          op0=ALU.mult, op1=ALU.subtract),
                     reads=[("pb", 4), ("tmp", m2)], writes=[("tmp", m3)])
                P.op("act", lambda e, m3=m3: e.activation(out=tmpv(m3), in_=tmpv(m3), func=AF.Sqrt, bias=EPSC[SUBLN_EPS], scale=1.0),
                     reads=[("tmp", m3), "epsc"], writes=[("tmp", m3)])
                P.op("dve", lambda e, m3=m3: e.reciprocal(out=tmpv(m3), in_=tmpv(m3)), reads=[("tmp", m3)], writes=[("tmp", m3)])
                for c in range(DC):
                    P.op("dve", lambda e, c=c, m1=m1: e.tensor_tensor(out=CT[:, c, :], in0=CT[:, c, :], in1=tmpv(m1), op=ALU.subtract),
                         reads=[("ct", c), ("tmp", m1)], writes=[("ct", c)])
                    P.op("dve", lambda e, c=c, m3=m3: e.scalar_tensor_tensor(out=CT[:, c, :], in0=CT[:, c, :], scalar=pc(f"lng{i}", c),
                                                                            in1=tmpv(m3), op0=ALU.mult, op1=ALU.mult),
                         reads=[("ct", c), ("tmp", m3), "par"], writes=[("ct", c)])
                    P.op("act", lambda e, c=c, tt=tt: e.activation(out=HB[:, c, tt * TT:(tt + 1) * TT], in_=CT[:, c, :], func=AF.Silu,
                                                                    bias=pc(f"lnb{i}", c)),
                         reads=[("ct", c), "par"], writes=[("h", c, tt)])
            for hf in range(2):
                P.barrier()
                proj_post(lambda kc, tt, hf=hf: (HB[:, kc, (hf * NTH + tt) * TT:(hf * NTH + tt + 1) * TT], ("h", kc, hf * NTH + tt)),
                          DC, pw2_d[ic], f"bpw2{i}", lambda c: pc(f"mpost{i}", c), hf)

        TABv = HS[:, SCR0:SCR0 + NTAB].bitcast(I16)
        KTv = HS[:, SCR0 + NTAB:SCR0 + NTAB + S]
        VHv = HS[:, SCR0 + NTAB + S:SCR0 + NTAB + 2 * S].rearrange("p (k e) -> p k e", e=128)
        QZv = HS[:, SCR0 + NTAB + 2 * S:SCR0 + NTAB + 2 * S + 2 * T].rearrange("p (m t) -> p m t", m=2)
        assert 2 * WSL >= 2 * S
        KTs = [KTv, WR[:, 0:S]]
        VHs = [VHv, WR[:, S:2 * S].rearrange("p (k e) -> p k e", e=128)]
        QZs = [QZv, FT[:, 12 * TT:12 * TT + T].bitcast(BF16).rearrange("p (m t) -> p m t", m=2)]
        SCv = lambda s: FT[:, s * TT:(s + 1) * TT]
        PTall = FT[:, 8 * TT:12 * TT].bitcast(BF16)
        PTv = lambda s: PTall[:, s * TT:(s + 1) * TT]
        STG = FT[:, 0:TT].bitcast(BF16)
        VW = min(512, D)
        STGV = FT[:, TT:TT + VW].bitcast(BF16)
        scale = 64 ** -0.5

        def attn(i, ia):
            P.barrier()
            li = lambda_init(i)
            dbase = 16 * i
            P.op("dve", lambda e: e.tensor_tensor(out=DER[:, dbase:dbase + 1], in0=pc(f"lq1{i}"), in1=pc(f"lk1{i}"), op=ALU.mult),
                 reads=["par"], writes=[("der", i)])
            P.op("dve", lambda e: e.tensor_tensor(out=DER[:, dbase + 1:dbase + 2], in0=pc(f"lq2{i}"), in1=pc(f"lk2{i}"), op=ALU.mult),
                 reads=["par"], writes=[("der", i)])
            P.op("pe", lambda e: e.matmul(pb[7][:, 0:2], lhsT=ONESF[:, :], rhs=DER[:, dbase:dbase + 2], start=True, stop=True),
                 reads=[("der", i), "onesf"], writes=[("pb", 7)])
            P.op("act", lambda e: e.activation(out=DER[:, dbase + 2:dbase + 4], in_=pb[7][:, 0:2], func=AF.Exp),
                 reads=[("pb", 7)], writes=[("der", i)])
            P.op("dve", lambda e: e.tensor_tensor(out=DER[:, dbase + 4:dbase + 5], in0=DER[:, dbase + 3:dbase + 4],
                                                  in1=DER[:, dbase + 2:dbase + 3], op=ALU.subtract),
                 reads=[("der", i)], writes=[("der", i)])
            P.op("dve", lambda e: e.tensor_scalar(out=DER[:, dbase + 5:dbase + 6], in0=DER[:, dbase + 4:dbase + 5], scalar1=-li, scalar2=None,
                                                  op0=ALU.add), reads=[("der", i)], writes=[("der", i)])
            P.op("dve", lambda e: e.tensor_scalar(out=DER[:, dbase + 6:dbase + 7], in0=pc(f"subg{i}"), scalar1=1.0 - li, scalar2=None,
                                                  op0=ALU.mult), reads=["par"], writes=[("der", i)])
            NEGLAM = DER[:, dbase + 5:dbase + 6]
            GSUB = DER[:, dbase + 6:dbase + 7]
            prenorm(f"mpre{i}", list(range(NT)), lambda c, n_: HB[:, c, n_ * TT:(n_ + 1) * TT])
            stg_ring, bank_ring = Ring(2), Ring(8)
            for off_w, dst, dres in ((0, q_loc, "q_loc"), (D, kt_loc, "kt_loc")):
                for c0 in range(0, D, GW):
                    w = min(GW, D - c0)
                    slot = wr_ring.next()
                    load_w(slot, 0, qkv_d[ia], off_w + c0, w)
                    for cl in range(w // 128):
                        c = c0 // 128 + cl
                        for tt in range(NT):
                            bk = bank_ring.next()
                            for kc in range(DC):
                                P.op("pe", lambda e, bk=bk, kc=kc, cl=cl, tt=tt, slot=slot, w=w: e.matmul(
                                    pb[bk][:, 0:TT], lhsT=wview(slot, 0, w)[:, kc, cl * 128:(cl + 1) * 128],
                                    rhs=HB[:, kc, tt * TT:(tt + 1) * TT], start=(kc == 0), stop=(kc == DC - 1)),
                                    reads=[("wr", slot, 0), ("h", kc, tt)], writes=[("pb", bk)], signal=(kc == DC - 1))
                            s = stg_ring.next()
                            eng = "act" if (tt % 2 == 0) else "dve"
                            if eng == "act":
                                P.op("act", lambda e, bk=bk, s=s: e.activation(out=STG[:, s * TT:(s + 1) * TT], in_=pb[bk][:, 0:TT], func=AF.Copy),
                                     reads=[("pb", bk)], writes=[("stg", s)])
                            else:
                                P.op("dve", lambda e, bk=bk, s=s: e.tensor_copy(out=STG[:, s * TT:(s + 1) * TT], in_=pb[bk][:, 0:TT]),
                                     reads=[("pb", bk)], writes=[("stg", s)])
                            P.dma("sp", lambda e, s=s, c=c, tt=tt, dst=dst: e.dma_start(
                                out=dst[c * 128:(c + 1) * 128, tt * TT:(tt + 1) * TT], in_=STG[:, s * TT:(s + 1) * TT]),
                                reads=[("stg", s)], writes=[dres])
            stgv_ring = Ring(2)
            for cg in range(D // VW):
                slot = wr_ring.next()
                load_w(slot, 0, qkv_d[ia], 2 * D + cg * VW, VW)
                for tk in range(T // 128):
                    bk = bank_ring.next()
                    for kc in range(DC):
                        P.op("pe", lambda e, bk=bk, kc=kc, tk=tk, slot=slot: e.matmul(
                            pb[bk][:, 0:VW], lhsT=HB[:, kc, tk * 128:(tk + 1) * 128], rhs=wview(slot, 0, VW)[:, kc, :],
                            start=(kc == 0), stop=(kc == DC - 1)),
                            reads=[("wr", slot, 0), ("h", kc, (tk * 128) // TT)], writes=[("pb", bk)], signal=(kc == DC - 1))
                    s = stgv_ring.next()
                    if tk % 2 == 0:
                        P.op("act", lambda e, bk=bk, s=s: e.activation(out=STGV[:, s * VW:(s + 1) * VW], in_=pb[bk][:, 0:VW], func=AF.Copy),
                             reads=[("pb", bk)], writes=[("stgv", s)])
                    else:
                        P.op("dve", lambda e, bk=bk, s=s: e.tensor_copy(out=STGV[:, s * VW:(s + 1) * VW], in_=pb[bk][:, 0:VW]),
                             reads=[("pb", bk)], writes=[("stgv", s)])
                    P.dma("sp", lambda e, s=s, tk=tk, cg=cg: e.dma_start(
                        out=v_loc[tk * 128:(tk + 1) * 128, cg * VW:(cg + 1) * VW], in_=STGV[:, s * VW:(s + 1) * VW]),
                        reads=[("stgv", s)], writes=["v_loc"])
            for h in range(H):
                P.collective(lambda e, h=h: e.collective_compute(
                    "AllGather", ALU.bypass, replica_groups=groups,
                    ins=[kt_loc[h * 128:(h + 1) * 128, :].opt()], outs=[kt_all[h * 256:(h + 1) * 256, :].opt()]),
                    reads=["kt_loc"], writes=["kt_all"])
            for j in range(NJ):
                P.collective(lambda e, j=j: e.collective_compute(
                    "AllGather", ALU.bypass, replica_groups=groups,
                    ins=[v_loc[j * CRV:(j + 1) * CRV, :].opt()], outs=[v_all[j * 2 * CRV:(j + 1) * 2 * CRV, :].opt()]),
                    reads=["v_loc"], writes=["v_all"])
            P.barrier()
            P.dma("sp", lambda e: e.dma_start(out=TABv, in_=tab_d[:, :]), writes=["tab"])
            for b_ in range(2):
                P.op("dve", lambda e, b_=b_: e.memset(QZs[b_][:, :, :], 0.0), writes=[("qz0", b_), ("qth", b_)])
            sc_ring, pt_ring, sb_ring = Ring(8), Ring(8), Ring(4)
            LOOK, LATE, pend, late = 6, min(12, (S // 128) // 2), [], []
            KPC = CRV // 128

            def head_loads(h):
                b_ = h % 2
                for r_ in range(2):
                    P.dma("sp", lambda e, r_=r_: e.dma_start(out=KTs[b_][:, r_ * T:(r_ + 1) * T],
                                                              in_=kt_all[h * 256 + r_ * 128: h * 256 + (r_ + 1) * 128, :]),
                          reads=["kt_all"], writes=[("kth", b_)])
                for m_ in range(2):
                    P.dma("sp", lambda e, m_=m_: e.dma_start(out=QZs[b_][m_ * 64:(m_ + 1) * 64, m_, :],
                                                              in_=q_loc[h * 128 + m_ * 64:h * 128 + (m_ + 1) * 64, :]),
                          reads=["q_loc", ("qz0", b_)], writes=[("qth", b_)])
                for r_ in range(2):
                    for j in range(NJ):
                        kt0 = r_ * (T // 128) + j * KPC
                        row0 = j * 2 * CRV + r_ * CRV
                        src = v_all[row0:row0 + CRV, h * 128:(h + 1) * 128].rearrange("(k p) e -> p k e", p=128)
                        P.dma("sp", lambda e, kt0=kt0, src=src: e.dma_start(out=VHs[b_][:, kt0:kt0 + KPC, :], in_=src),
                              reads=["v_all"], writes=[("vh", b_)])

            head_loads(0)
            for h in range(H):
                if h + 1 < H:
                    head_loads(h + 1)
                hb = h % 2
                KTc, VHc, QZc = KTs[hb], VHs[hb], QZs[hb]
                slope = 2.0 ** (-8.0 * (h + 1) / H)
                nk = S // 128
                for qt in range(NT):
                    osl = {}
                    for kt in range(nk):
                        for m in range(2):
                            ob, db = 2 * m, 2 * m + 1
                            sbk = 4 + sb_ring.next()
                            P.op("pe", lambda e, m=m, kt=kt, qt=qt, sbk=sbk, KTc=KTc, QZc=QZc: e.matmul(
                                pb[sbk][:, 0:TT], lhsT=KTc[:, kt * 128:(kt + 1) * 128],
                                rhs=QZc[:, m, qt * TT:(qt + 1) * TT], start=True, stop=True),
                                reads=[("kth", hb), ("qth", hb)], writes=[("pb", sbk)])
                            s1 = sc_ring.next()
                            toff = qt * TT - kt * 128 + (S - 128)
                            P.op("dve", lambda e, s1=s1, sbk=sbk, toff=toff, slope=slope: e.scalar_tensor_tensor(
                                out=SCv(s1), in0=TABv[:, toff:toff + TT], scalar=-slope / scale, in1=pb[sbk][:, 0:TT],
                                op0=ALU.mult, op1=ALU.add), reads=["tab", ("pb", sbk)], writes=[("sc", s1)])
                            s2 = pt_ring.next()
                            P.op("act", lambda e, s1=s1, s2=s2: e.activation(out=PTv(s2), in_=SCv(s1), func=AF.Exp, scale=scale),
                                 reads=[("sc", s1)], writes=[("pt", s2)])

                            def pv(s2=s2, kt=kt, ob=ob, db=db, m=m, qt=qt, osl=osl, h=h, VHc=VHc, hb=hb):
                                P.op("pe", lambda e: e.matmul(
                                    pb[ob][:, 0:TT], lhsT=VHc[:, kt, :], rhs=PTv(s2), start=(kt == 0), stop=(kt == nk - 1)),
                                    reads=[("vh", hb), ("pt", s2)], writes=[("pb", ob)], signal=False)
                                P.op("pe", lambda e: e.matmul(
                                    pb[db][:, 0:TT], lhsT=ONESB[:, :], rhs=PTv(s2), start=(kt == 0), stop=(kt == nk - 1)),
                                    reads=["onesb", ("pt", s2)], writes=[("pb", db), ("pb", ob)], signal=True)
                                if kt != nk - 1:
                                    return
                                rc, om = tmp_ring.next(), tmp_ring.next()
                                P.op("act", lambda e: e.activation(out=tmpv(rc), in_=pb[db][:, 0:TT], func=AF.Ln),
                                     reads=[("pb", db)], writes=[("tmp", rc)])
                                P.op("act", lambda e: e.activation(out=tmpv(rc), in_=tmpv(rc), func=AF.Exp, scale=-1.0),
                                     reads=[("tmp", rc)], writes=[("tmp", rc)])
                                P.op("dve", lambda e: e.tensor_tensor(out=tmpv(om), in0=pb[ob][:, 0:TT], in1=tmpv(rc), op=ALU.mult),
                                     reads=[("pb", ob), ("tmp", rc)], writes=[("tmp", om)])
                                osl[m] = om
                                if m == 0:
                                    return

                                def tail(o0=osl[0], o1=osl[1]):
                                    oh = tmp_ring.next()
                                    P.op("dve", lambda e: e.scalar_tensor_tensor(
                                        out=tmpv(oh), in0=tmpv(o1), scalar=NEGLAM, in1=tmpv(o0), op0=ALU.mult, op1=ALU.add),
                                        reads=[("tmp", o0), ("tmp", o1), ("der", i)], writes=[("tmp", oh)])
                                    r = rms_rstd(lambda c: (tmpv(oh), ("tmp", oh)), 1, SUBLN_EPS, 1.0 / 128, 4 + sb_ring.next(), use_ln=True)
                                    P.op("dve", lambda e: e.scalar_tensor_tensor(
                                        out=HB[:, h, qt * TT:(qt + 1) * TT], in0=tmpv(oh), scalar=GSUB, in1=tmpv(r), op0=ALU.mult, op1=ALU.mult),
                                        reads=[("tmp", oh), ("tmp", r), ("der", i)], writes=[("h", h, qt)])
                                late.append([LATE, tail])

                            pend.append(pv)
                            while len(pend) > LOOK:
                                pend.pop(0)()
                            for it in list(late):
                                it[0] -= 1
                                if it[0] <= 0:
                                    late.remove(it)
                                    it[1]()
                while pend:
                    pend.pop(0)()
                while late:
                    late.pop(0)[1]()
            for hf in range(2):
                P.barrier()
                proj_post(lambda kc, tt, hf=hf: (HB[:, kc, (hf * NTH + tt) * TT:(hf * NTH + tt + 1) * TT], ("h", kc, hf * NTH + tt)),
                          DC, wo_d[ia], None, lambda c: pc(f"mpost{i}", c), hf)

        ic = ia = 0
        for i in range(L):
            ffn(i, 0)
            if i % 2 == 0:
                conv(i, ic)
                ic += 1
            else:
                attn(i, ia)
                ia += 1
            ffn(i, 1)
        P.barrier()
        for c in range(DC):
            P.dma("sp", lambda e, c=c: e.dma_start(out=yT_d[c * 128:(c + 1) * 128, :], in_=Xv[:, c, :]),
                  reads=[("x", c, t) for t in range(NT)], writes=[("y", c)])
        P.barrier()

        @block.tensor
        def _(e):
            P.replay("pe", e, sems)

        @block.scalar
        def _(e):
            P.replay("act", e, sems)

        @block.vector
        def _(e):
            P.replay("dve", e, sems)

        @block.gpsimd
        def _(e):
            P.replay("pool", e, sems)

        @block.sync
        def _(e):
            P.replay("sp", e, sems)
    return nc


def pack_inputs(cfg, inp, core):
    D, T, DC, L, S = cfg.D, cfg.T, cfg.DC, cfg.L, cfg.S
    poff, NP = param_layout(cfg)
    b, r = core // 2, core % 2
    par = np.zeros((128, NP), np.float32)
    par[:, poff["ident"]:poff["ident"] + 128] = np.eye(128, dtype=np.float32)
    par[:, poff["mL"]] = 1.0 if r == 1 else 0.0
    par[:, poff["mR"]] = 1.0 if r == 0 else 0.0

    def vec(name, v):
        v = np.asarray(v, np.float32)
        n = v.shape[0] // 128
        par[:, poff[name]:poff[name] + n] = v.reshape(n, 128).T

    ic = ia = 0
    for i in range(L):
        for j in range(2):
            vec(f"fpre{i}{j}", inp["ffn_norm_pre"][i, j])
            vec(f"fpost{i}{j}", inp["ffn_norm_post"][i, j])
        vec(f"mpre{i}", inp["mix_norm_pre"][i])
        vec(f"mpost{i}", inp["mix_norm_post"][i])
        if i % 2 == 0:
            vec(f"bpw1{i}", inp["conv_b_pw1"][ic])
            wdw = np.asarray(inp["conv_w_dw"][ic], np.float32)
            o = poff[f"wdw{i}"]
            par[:, o:o + DC * CONVW] = wdw.T.reshape(DC, 128, CONVW).transpose(1, 0, 2).reshape(128, DC * CONVW)
            vec(f"bdw{i}", inp["conv_b_dw"][ic])
            vec(f"lng{i}", inp["conv_ln_g"][ic])
            vec(f"lnb{i}", inp["conv_ln_b"][ic])
            vec(f"bpw2{i}", inp["conv_b_pw2"][ic])
            ic += 1
        else:
            par[:, poff[f"subg{i}"]] = np.asarray(inp["attn_subln_g"][ia], np.float32)
            for nm, key in (("lq1", "attn_lambda_q1"), ("lk1", "attn_lambda_k1"), ("lq2", "attn_lambda_q2"), ("lk2", "attn_lambda_k2")):
                par[0:64, poff[f"{nm}{i}"]] = np.asarray(inp[key][ia], np.float32)
            ia += 1
    m = np.arange(cfg.NTAB, dtype=np.int64)[None, :]
    p = np.arange(128, dtype=np.int64)[:, None]
    tab = np.abs(m - (S - 128) + r * T - p).astype(np.int16)
    xT = np.ascontiguousarray(np.asarray(inp["x"], np.float32)[b, r * T:(r + 1) * T, :].T)
    f32 = lambda a: np.ascontiguousarray(np.asarray(a, np.float32))
    return {"xT": xT, "par": par, "tab": tab, "wg": f32(inp["ffn_w_gate"]), "wu": f32(inp["ffn_w_up"]), "wd": f32(inp["ffn_w_down"]),
            "pw1": f32(inp["conv_w_pw1"]), "pw2": f32(inp["conv_w_pw2"]), "qkv": f32(inp["attn_w_qkv"]), "wo": f32(inp["attn_w_o"])}


def kernel(**inputs):
    cfg = Cfg()
    nc = build(cfg)
    in_maps = [pack_inputs(cfg, inputs, c) for c in range(cfg.ncores)]
    res = run_bass_kernel_spmd(nc, in_maps, core_ids=list(range(cfg.ncores)))
    B = cfg.ncores // 2
    out = np.empty((B, cfg.S, cfg.D), np.float32)
    for c in range(cfg.ncores):
        out[c // 2, (c % 2) * cfg.T:(c % 2 + 1) * cfg.T, :] = np.asarray(res.results[c]["yT"]).T
    return out
```

```python
import math
from contextlib import ExitStack
import numpy as np
import concourse.bass as bass
import concourse.mybir as mybir
from concourse.bass_utils import run_bass_kernel_spmd

F32, BF16, I16 = mybir.dt.float32, mybir.dt.bfloat16, mybir.dt.int16
AF = mybir.ActivationFunctionType
ALU = mybir.AluOpType
RMS_EPS, SUBLN_EPS, LN_EPS = 1e-6, 1e-5, 1e-5
CONVW = 31
NDS = 24


class Cfg:
    def __init__(self, D=1024, F=2816, T=2048, TT=512, L=4, ncores=8):
        self.D, self.F, self.T, self.TT, self.L, self.ncores = D, F, T, TT, L, ncores
        self.DC, self.FC, self.NT, self.S, self.H = D // 128, F // 128, T // TT, 2 * T, D // 128
        self.TH = T // 2
        self.NTH = self.TH // TT
        self.GW = 256
        self.NDCL = min(4, self.DC)
        self.PW = self.NDCL * 128
        self.NTAB = 3 * T - 128
        assert self.NDCL * self.NTH <= 8 and self.NTH * TT == self.TH


def param_layout(cfg):
    DC, L = cfg.DC, cfg.L
    off, n = {}, 0

    def add(name, w):
        nonlocal n
        off[name] = n
        n += w

    add("ident", 128)
    add("mL", 1)
    add("mR", 1)
    for i in range(L):
        for j in range(2):
            add(f"fpre{i}{j}", DC)
            add(f"fpost{i}{j}", DC)
        add(f"mpre{i}", DC)
        add(f"mpost{i}", DC)
        if i % 2 == 0:
            add(f"bpw1{i}", 2 * DC)
            add(f"wdw{i}", DC * CONVW)
            for nm in ("bdw", "lng", "lnb", "bpw2"):
                add(f"{nm}{i}", DC)
        else:
            for nm in ("subg", "lq1", "lk1", "lq2", "lk2"):
                add(f"{nm}{i}", 1)
    return off, n


class Prog:
    def __init__(self):
        self.names = ["pe", "act", "dve", "pool", "sp"]
        self.ops = {k: [] for k in self.names}
        self.cnt = {k: 0 for k in ("pe", "act", "dve")}
        self.seen = {k: {} for k in self.names}
        self.lastw, self.readers = {}, {}
        self.dma_tot = [0] * NDS
        self.dma_next = {"sp": 0, "pool": 0}
        self.ncc = 0

    def _deps(self, eng, reads, writes, extra=()):
        deps = {}

        def add(tok):
            if tok is not None:
                deps[tok[0]] = max(deps.get(tok[0], 0), tok[1])

        for r in reads:
            add(self.lastw.get(r))
        for w in writes:
            add(self.lastw.get(w))
            for kv in self.readers.get(w, {}).items():
                add(kv)
        for t in extra:
            add(t)
        waits = []
        for k, v in deps.items():
            if k == eng and eng == "pe":
                continue
            if self.seen[eng].get(k, 0) >= v:
                continue
            self.seen[eng][k] = v
            waits.append((k, v))
        return waits

    def _record(self, tok, reads, writes):
        for r in reads:
            d = self.readers.setdefault(r, {})
            d[tok[0]] = max(d.get(tok[0], 0), tok[1])
        for w in writes:
            self.lastw[w] = tok
            self.readers[w] = {}

    def op(self, eng, fn, reads=(), writes=(), signal=True):
        waits = self._deps(eng, reads, writes)
        if signal:
            self.cnt[eng] += 1
            tok = (eng, self.cnt[eng])
        else:
            tok = (eng, self.cnt[eng] + 1)
        self.ops[eng].append((waits, fn, eng if signal else None, 1))
        self._record(tok, reads, writes)

    def dma(self, q, fn, reads=(), writes=()):
        half = NDS // 2
        j = self.dma_next[q]
        self.dma_next[q] = (j + 1) % half
        i = j + (half if q == "pool" else 0)
        key = ("d", i)
        extra = [(key, self.dma_tot[i])] if self.dma_tot[i] else []
        waits = self._deps(q, reads, writes, extra)
        self.dma_tot[i] += 16
        self.ops[q].append((waits, fn, key, 16))
        self._record((key, self.dma_tot[i]), reads, writes)

    def collective(self, fn, reads=(), writes=()):
        key = ("c", self.ncc)
        self.ncc += 1
        waits = self._deps("pool", reads, writes)
        self.ops["pool"].append((waits, fn, key, None))
        self._record((key, 1), reads, writes)

    def barrier(self):
        cur = [(k, v) for k, v in self.cnt.items() if v]
        cur += [(("d", i), t) for i, t in enumerate(self.dma_tot) if t]
        cur += [(("c", j), 1) for j in range(self.ncc)]
        for e in self.names:
            waits = []
            for k, v in cur:
                if k == e and e == "pe":
                    continue
                if self.seen[e].get(k, 0) >= v:
                    continue
                self.seen[e][k] = v
                waits.append((k, v))
            if waits:
                self.ops[e].append((waits, None, None, 0))

    def replay(self, name, e, sems):
        for waits, fn, inckey, amt in self.ops[name]:
            if fn is None or name not in ("act", "dve"):
                for k, v in waits:
                    e.wait_ge(sems[k], v)
                if fn is not None:
                    ins = fn(e)
                    if inckey is not None:
                        if amt is None:
                            ins.then_inc(sems[inckey])
                        else:
                            ins.then_inc(sems[inckey], amt)
                continue
            for k, v in waits[:-1]:
                e.wait_ge(sems[k], v)
            ins = fn(e)
            if waits:
                ins._wait_ge(sems[waits[-1][0]], waits[-1][1])
            if inckey is not None:
                if amt is None:
                    ins.then_inc(sems[inckey])
                else:
                    ins.then_inc(sems[inckey], amt)


class Ring:
    def __init__(self, n):
        self.n, self.i = n, 0

    def next(self):
        v = self.i
        self.i = (self.i + 1) % self.n
        return v


def lambda_init(i):
    return 0.8 - 0.6 * math.exp(-0.3 * i)


def build(cfg):
    D, F, T, TT, L = cfg.D, cfg.F, cfg.T, cfg.TT, cfg.L
    DC, FC, NT, S, H, TH, NTH, GW, NDCL, PW = cfg.DC, cfg.FC, cfg.NT, cfg.S, cfg.H, cfg.TH, cfg.NTH, cfg.GW, cfg.NDCL, cfg.PW
    NTAB = cfg.NTAB
    NC2, NA2 = (L + 1) // 2, L // 2
    poff, NP = param_layout(cfg)
    nc = bass.Bass("TRN2", target_bir_lowering=False)
    dt = nc.dram_tensor
    xT_d = dt("xT", [D, T], F32, kind="ExternalInput").ap()
    par_d = dt("par", [128, NP], F32, kind="ExternalInput").ap()
    tab_d = dt("tab", [128, NTAB], I16, kind="ExternalInput").ap()
    wg_d = dt("wg", [L, 2, D, F], F32, kind="ExternalInput").ap()
    wu_d = dt("wu", [L, 2, D, F], F32, kind="ExternalInput").ap()
    wd_d = dt("wd", [L, 2, F, D], F32, kind="ExternalInput").ap()
    pw1_d = dt("pw1", [NC2, D, 2 * D], F32, kind="ExternalInput").ap()
    pw2_d = dt("pw2", [NC2, D, D], F32, kind="ExternalInput").ap()
    qkv_d = dt("qkv", [max(NA2, 1), D, 3 * D], F32, kind="ExternalInput").ap()
    wo_d = dt("wo", [max(NA2, 1), D, D], F32, kind="ExternalInput").ap()
    yT_d = dt("yT", [D, T], F32, kind="ExternalOutput").ap()
    q_loc = dt("q_loc", [D, T], BF16).ap()
    kt_loc = dt("kt_loc", [D, T], BF16).ap()
    kt_all = dt("kt_all", [2 * D, T], BF16).ap()
    v_loc = dt("v_loc", [T, D], BF16).ap()
    v_all = dt("v_all", [2 * T, D], BF16).ap()
    halo_loc = dt("halo_loc", [D, 32], BF16).ap()
    halo_all = dt("halo_all", [2 * D, 32], BF16).ap()
    groups = [[2 * i, 2 * i + 1] for i in range(cfg.ncores // 2)]
    CRV = getattr(cfg, 'CRV', None) or min(T, max(128, 262144 // D))
    NJ = T // CRV

    P = Prog()
    HBN = DC * T
    SCRN = max(DC * (T + 32), FC * TH - HBN // 2, NTAB + S + S + 2 * T) + 16
    FTN = max(DC * TH, DC * TT + CONVW * 128, 12 * TT + T)
    WSL = max(2 * DC * max(GW, min(512, D) // 2), CONVW * 128)

    with ExitStack() as es:
        sb = lambda name, shape, d: es.enter_context(nc.sbuf_tensor(name, shape, d))
        X = sb("X", [128, DC * T], F32)
        HS = sb("HS", [128, HBN + SCRN], BF16)
        FT = sb("FT", [128, FTN], F32)
        WR = sb("WR", [128, 2 * WSL], BF16)
        WDR = sb("WDR", [128, 4 * PW], BF16)
        TMP = sb("TMP", [128, 6 * TT], F32)
        SQ = sb("SQ", [128, 2 * TT], BF16)
        PAR = sb("PAR", [128, NP], F32)
        DER = sb("DER", [128, 16 * L + 8], F32)
        ONESB = sb("ONESB", [128, 128], BF16)
        ONESF = sb("ONESF", [128, 128], F32)
        GH = sb("GH", [128, L * 2 * DC], F32)
        EPST = sb("EPST", [128, 4], F32)
        pb = [es.enter_context(nc.psum_tensor(f"pb{i}", [128, 512], F32)) for i in range(8)]
        sems = {}
        for k in ("pe", "act", "dve"):
            sems[k] = es.enter_context(nc.semaphore(f"s_{k}"))
        for i in range(NDS):
            sems[("d", i)] = es.enter_context(nc.semaphore(f"s_d{i}"))
        for j in range((L // 2) * (H + T // CRV) + (L + 1) // 2 + 1):
            sems[("c", j)] = es.enter_context(nc.semaphore(f"s_c{j}"))
        block = es.enter_context(nc.Block())

        Xv = X[:, :].rearrange("p (c t) -> p c t", c=DC)
        HB = HS[:, 0:HBN].rearrange("p (c t) -> p c t", c=DC)
        tmp_ring, sq_ring, wr_ring, wdr_ring = Ring(6), Ring(2), Ring(2), Ring(4)
        pc = lambda name, k=0: PAR[:, poff[name] + k: poff[name] + k + 1]
        tmpv = lambda s: TMP[:, s * TT:(s + 1) * TT]
        sqv = lambda s: SQ[:, s * TT:(s + 1) * TT]

        P.dma("sp", lambda e: e.dma_start(out=PAR[:, :], in_=par_d[:, :]), writes=["par"])
        for c in range(DC):
            P.dma("sp", lambda e, c=c: e.dma_start(out=Xv[:, c, :], in_=xT_d[c * 128:(c + 1) * 128, :]),
                  writes=[("x", c, t) for t in range(NT)])
        P.op("dve", lambda e: e.memset(ONESB[:, :], 1.0), writes=["onesb"])
        P.op("dve", lambda e: e.memset(ONESF[:, :], 1.0), writes=["onesf"])
        for i in range(L):
            for j in range(2):
                o = (i * 2 + j) * DC
                P.op("dve", lambda e, o=o, i=i, j=j: e.tensor_scalar(
                    out=GH[:, o:o + DC], in0=PAR[:, poff[f"fpost{i}{j}"]:poff[f"fpost{i}{j}"] + DC],
                    scalar1=0.5, scalar2=None, op0=ALU.mult), reads=["par"], writes=["gh"])

        def rms_rstd(src_fn, n, eps, inv_n, bank, use_ln=False):
            for c in range(n):
                ap, res = src_fn(c)
                s = sq_ring.next()
                P.op("act", lambda e, ap=ap, s=s: e.activation(out=sqv(s), in_=ap, func=AF.Square),
                     reads=[res], writes=[("sq", s)])
                P.op("pe", lambda e, s=s, c=c: e.matmul(pb[bank][:, 0:TT], lhsT=ONESB[:, :], rhs=sqv(s),
                                                       start=(c == 0), stop=(c == n - 1)),
                     reads=[("sq", s), "onesb"], writes=[("pb", bank)], signal=True)
            r = tmp_ring.next()
            if use_ln:
                P.op("act", lambda e, r=r: e.activation(out=tmpv(r), in_=pb[bank][:, 0:TT], func=AF.Ln,
                                                       bias=EPSC[eps], scale=inv_n),
                     reads=[("pb", bank), "epsc"], writes=[("tmp", r)])
                P.op("act", lambda e, r=r: e.activation(out=tmpv(r), in_=tmpv(r), func=AF.Exp, scale=-0.5),
                     reads=[("tmp", r)], writes=[("tmp", r)])
                return r
            P.op("act", lambda e, r=r: e.activation(out=tmpv(r), in_=pb[bank][:, 0:TT], func=AF.Sqrt,
                                                   bias=EPSC[eps], scale=inv_n),
                 reads=[("pb", bank), "epsc"], writes=[("tmp", r)])
            P.op("dve", lambda e, r=r: e.reciprocal(out=tmpv(r), in_=tmpv(r)), reads=[("tmp", r)], writes=[("tmp", r)])
            return r

        EPSC = {RMS_EPS: EPST[:, 0:1], SUBLN_EPS: EPST[:, 1:2], 0.0: EPST[:, 2:3]}
        P.op("dve", lambda e: e.memset(EPST[:, 0:1], RMS_EPS), writes=["epsc"])
        P.op("dve", lambda e: e.memset(EPST[:, 1:2], SUBLN_EPS), writes=["epsc"])
        P.op("dve", lambda e: e.memset(EPST[:, 2:3], 0.0), writes=["epsc"])

        def prenorm(gname, tiles, hdst):
            for n_, tl in enumerate(tiles):
                tok = tl * TT
                r = rms_rstd(lambda c, tok=tok, tl=tl: (Xv[:, c, tok:tok + TT], ("x", c, tl)), DC, RMS_EPS, 1.0 / D, n_ % 8)
                for c in range(DC):
                    P.op("dve", lambda e, c=c, tok=tok, r=r, n_=n_: e.scalar_tensor_tensor(
                        out=hdst(c, n_), in0=Xv[:, c, tok:tok + TT], scalar=pc(gname, c), in1=tmpv(r),
                        op0=ALU.mult, op1=ALU.mult),
                        reads=[("x", c, tl), ("tmp", r), "par"], writes=[("h", c, n_)])

        def wview(slot, part, w):
            base = slot * WSL + part * (WSL // 2)
            return WR[:, base:base + DC * w].rearrange("p (k f) -> p k f", k=DC)

        def load_w(slot, part, w2d, c0, w):
            src = w2d.rearrange("(k p) f -> p k f", p=128)[:, :, c0:c0 + w]
            P.dma("pool", lambda e: e.dma_start(out=wview(slot, part, w), in_=src), writes=[("wr", slot, part)])

        def gated_proj(hv, w_act, off_act, w_lin, off_lin, ncols, ntiles, func, b_act, b_lin, out_fn):
            pr = Ring(4)
            for c0 in range(0, ncols, GW):
                w = min(GW, ncols - c0)
                slot = wr_ring.next()
                load_w(slot, 0, w_act, off_act + c0, w)
                load_w(slot, 1, w_lin, off_lin + c0, w)
                for cl in range(w // 128):
                    c = c0 // 128 + cl
                    for tt in range(ntiles):
                        p2 = pr.next()
                        ba, bl = 2 * p2, 2 * p2 + 1
                        for part, bk in ((0, ba), (1, bl)):
                            for kc in range(DC):
                                P.op("pe", lambda e, part=part, bk=bk, kc=kc, cl=cl, tt=tt, slot=slot, w=w: e.matmul(
                                    pb[bk][:, 0:TT], lhsT=wview(slot, part, w)[:, kc, cl * 128:(cl + 1) * 128],
                                    rhs=hv[:, kc, tt * TT:(tt + 1) * TT], start=(kc == 0), stop=(kc == DC - 1)),
                                    reads=[("wr", slot, part), ("h", kc, tt)], writes=[("pb", bk)], signal=(kc == DC - 1))
                        s = tmp_ring.next()
                        if b_act is None:
                            P.op("act", lambda e, ba=ba, s=s: e.activation(out=tmpv(s), in_=pb[ba][:, 0:TT], func=func),
                                 reads=[("pb", ba)], writes=[("tmp", s)])
                        else:
                            P.op("act", lambda e, ba=ba, s=s, c=c: e.activation(out=tmpv(s), in_=pb[ba][:, 0:TT], func=func,
                                                                               bias=b_act(c)),
                                 reads=[("pb", ba), "par"], writes=[("tmp", s)])
                        oap, ores = out_fn(c, tt)
                        if b_lin is None:
                            P.op("dve", lambda e, bl=bl, s=s, oap=oap: e.tensor_tensor(out=oap, in0=pb[bl][:, 0:TT], in1=tmpv(s), op=ALU.mult),
                                 reads=[("pb", bl), ("tmp", s)], writes=[ores])
                        else:
                            P.op("dve", lambda e, bl=bl, s=s, oap=oap, c=c: e.scalar_tensor_tensor(
                                out=oap, in0=pb[bl][:, 0:TT], scalar=b_lin(c), in1=tmpv(s), op0=ALU.add, op1=ALU.mult),
                                reads=[("pb", bl), ("tmp", s), "par"], writes=[ores])

        FTv = FT[:, 0:DC * TH].rearrange("p (c t) -> p c t", c=DC)

        def proj_post(in_fn, KC, w2d, bias, gain_fn, hf, do=("mm", "post")):
            for ps in range(D // PW if "mm" in do else 0):
                for kc in range(KC):
                    slot = wdr_ring.next()
                    src = w2d[kc * 128:(kc + 1) * 128, ps * PW:(ps + 1) * PW]
                    P.dma("pool", lambda e, slot=slot, src=src: e.dma_start(out=WDR[:, slot * PW:(slot + 1) * PW], in_=src),
                          writes=[("wdr", slot)])
                    for dl in range(NDCL):
                        for tt in range(NTH):
                            iap, ires = in_fn(kc, tt)
                            bk = dl * NTH + tt
                            P.op("pe", lambda e, slot=slot, dl=dl, iap=iap, bk=bk, kc=kc: e.matmul(
                                pb[bk][:, 0:TT], lhsT=WDR[:, slot * PW + dl * 128: slot * PW + (dl + 1) * 128], rhs=iap,
                                start=(kc == 0), stop=(kc == KC - 1)),
                                reads=[("wdr", slot), ires], writes=[("pb", bk)],
                                signal=(kc == KC - 1 or (dl == NDCL - 1 and tt == NTH - 1)))
                for dl in range(NDCL):
                    dc = ps * NDCL + dl
                    for tt in range(NTH):
                        bk = dl * NTH + tt
                        dst = FTv[:, dc, tt * TT:(tt + 1) * TT]
                        if bias is not None:
                            P.op("act", lambda e, bk=bk, dst=dst, dc=dc: e.activation(out=dst, in_=pb[bk][:, 0:TT], func=AF.Identity,
                                                                                   bias=pc(bias, dc)),
                                 reads=[("pb", bk), "par"], writes=[("ft", dc, tt)])
                        elif (dl + tt) % 2 == 0:
                            P.op("act", lambda e, bk=bk, dst=dst: e.activation(out=dst, in_=pb[bk][:, 0:TT], func=AF.Copy),
                                 reads=[("pb", bk)], writes=[("ft", dc, tt)])
                        else:
                            P.op("dve", lambda e, bk=bk, dst=dst: e.tensor_copy(out=dst, in_=pb[bk][:, 0:TT]),
                                 reads=[("pb", bk)], writes=[("ft", dc, tt)])
            for tt in range(NTH if "post" in do else 0):
                tl = hf * NTH + tt
                tok = tl * TT
                r = rms_rstd(lambda c, tt=tt: (FTv[:, c, tt * TT:(tt + 1) * TT], ("ft", c, tt)), DC, RMS_EPS, 1.0 / D, tt)
                for c in range(DC):
                    f_ap = FTv[:, c, tt * TT:(tt + 1) * TT]
                    P.op("dve", lambda e, f_ap=f_ap, c=c, r=r: e.scalar_tensor_tensor(
                        out=f_ap, in0=f_ap, scalar=gain_fn(c), in1=tmpv(r), op0=ALU.mult, op1=ALU.mult),
                        reads=[("ft", c, tt), ("tmp", r), "par", "gh"], writes=[("ft", c, tt)])
                    P.op("dve", lambda e, f_ap=f_ap, c=c, tok=tok: e.tensor_tensor(
                        out=Xv[:, c, tok:tok + TT], in0=Xv[:, c, tok:tok + TT], in1=f_ap, op=ALU.add),
                        reads=[("ft", c, tt), ("x", c, tl)], writes=[("x", c, tl)])

        A_all = HS[:, HBN // 2: HBN // 2 + FC * TH].rearrange("p (c t) -> p c t", c=FC)
        HBh = HS[:, 0:DC * TH].rearrange("p (c t) -> p c t", c=DC)

        def ffn(i, j):
            go = (i * 2 + j) * DC
            pre = lambda hf: prenorm(f"fpre{i}{j}", [hf * NTH + tt for tt in range(NTH)], lambda c, n_: HBh[:, c, n_ * TT:(n_ + 1) * TT])
            gu = lambda: gated_proj(HBh, wg_d[i, j], 0, wu_d[i, j], 0, F, NTH, AF.Silu, None, None,
                                    lambda c, tt: (A_all[:, c, tt * TT:(tt + 1) * TT], ("A", c, tt)))
            down = lambda hf, do: proj_post(lambda kc, tt: (A_all[:, kc, tt * TT:(tt + 1) * TT], ("A", kc, tt)), FC, wd_d[i, j], None,
                                            lambda c: GH[:, go + c:go + c + 1], hf, do)
            P.barrier()
            pre(0)
            gu()
            pre(1)
            down(0, ("mm",))
            gu()
            down(0, ("post",))
            down(1, ("mm",))
            down(1, ("post",))

        SCR0 = HBN
        Uv = HS[:, SCR0:SCR0 + DC * (T + 32)].rearrange("p (c t) -> p c t", c=DC)
        CT = FT[:, 0:DC * TT].rearrange("p (c t) -> p c t", c=DC)

        def conv(i, ic):
            P.barrier()
            prenorm(f"mpre{i}", list(range(NT)), lambda c, n_: HB[:, c, n_ * TT:(n_ + 1) * TT])
            gated_proj(HB, pw1_d[ic], D, pw1_d[ic], 0, D, NT, AF.Sigmoid, lambda c: pc(f"bpw1{i}", DC + c), lambda c: pc(f"bpw1{i}", c),
                       lambda c, tt: (Uv[:, c, 16 + tt * TT:16 + (tt + 1) * TT], ("u", c, tt)))
            ures = [("u", c, t) for c in range(DC) for t in range(NT)]
            hl = halo_loc.rearrange("(c p) t -> p c t", p=128)
            P.dma("sp", lambda e: e.dma_start(out=hl[:, :, 0:16], in_=Uv[:, :, 16:32]), reads=ures, writes=["halo_loc"])
            P.dma("sp", lambda e: e.dma_start(out=hl[:, :, 16:32], in_=Uv[:, :, T:T + 16]), reads=ures, writes=["halo_loc"])
            P.collective(lambda e: e.collective_compute("AllGather", ALU.bypass, replica_groups=groups,
                                                        ins=[halo_loc.opt()], outs=[halo_all.opt()]),
                         reads=["halo_loc"], writes=["halo_all"])
            ha = halo_all.rearrange("(r c p) t -> r p c t", p=128, c=DC)
            P.dma("sp", lambda e: e.dma_start(out=Uv[:, :, 0:16], in_=ha[0][:, :, 16:32]), reads=["halo_all"], writes=["uL"])
            P.dma("sp", lambda e: e.dma_start(out=Uv[:, :, T + 16:T + 32], in_=ha[1][:, :, 0:16]), reads=["halo_all"], writes=["uR"])
            P.op("dve", lambda e: e.tensor_scalar(out=Uv[:, :, 0:16], in0=Uv[:, :, 0:16], scalar1=pc("mL"), scalar2=None, op0=ALU.mult),
                 reads=["uL", "par"], writes=["uL"])
            P.op("dve", lambda e: e.tensor_scalar(out=Uv[:, :, T + 16:T + 32], in0=Uv[:, :, T + 16:T + 32], scalar1=pc("mR"), scalar2=None,
                                                  op0=ALU.mult), reads=["uR", "par"], writes=["uR"])
            P.barrier()
            DGs = [WR[:, sl * WSL:sl * WSL + CONVW * 128].rearrange("p (j m) -> p j m", j=CONVW) for sl in range(2)]
            dg_ring = Ring(2)
            CBv = lambda s: SQ[:, s * TT:(s + 1) * TT]
            for tt in range(NT):
                for c in range(DC):
                    dsl = dg_ring.next()
                    DGv = DGs[dsl]
                    for j in range(CONVW):
                        P.op("dve", lambda e, c=c, j=j, DGv=DGv: e.tensor_scalar(
                            out=DGv[:, j, :], in0=PAR[:, poff["ident"]:poff["ident"] + 128],
                            scalar1=pc(f"wdw{i}", c * CONVW + j), scalar2=None, op0=ALU.mult),
                            reads=["par"], writes=[("dg", dsl)])
                    bk = c % 4
                    for j in range(CONVW):
                        P.op("pe", lambda e, c=c, j=j, bk=bk, tt=tt, DGv=DGv: e.matmul(
                            pb[bk][:, 0:TT], lhsT=DGv[:, j, :], rhs=Uv[:, c, tt * TT + j + 1: tt * TT + j + 1 + TT],
                            start=(j == 0), stop=(j == CONVW - 1)),
                            reads=[("dg", dsl), "uL", "uR"] + [("u", c, t) for t in range(max(0, tt - 1), min(NT, tt + 2))],
                            writes=[("pb", bk)], signal=(j == CONVW - 1))
                    P.op("act", lambda e, c=c, bk=bk: e.activation(out=CT[:, c, :], in_=pb[bk][:, 0:TT], func=AF.Identity,
                                                                    bias=pc(f"bdw{i}", c)),
                         reads=[("pb", bk), "par"], writes=[("ct", c)])
                    s = sq_ring.next()
                    P.op("act", lambda e, c=c, s=s: e.activation(out=sqv(s), in_=CT[:, c, :], func=AF.Square),
                         reads=[("ct", c)], writes=[("sq", s)])
                    P.op("pe", lambda e, c=c, s=s: e.matmul(pb[4][:, 0:TT], lhsT=ONESB[:, :], rhs=sqv(s), start=(c == 0), stop=(c == DC - 1)),
                         reads=[("sq", s), "onesb"], writes=[("pb", 4)], signal=True)
                    s2 = sq_ring.next()
                    P.op("act", lambda e, c=c, s2=s2: e.activation(out=sqv(s2), in_=CT[:, c, :], func=AF.Copy), reads=[("ct", c)], writes=[("sq", s2)])
                    P.op("pe", lambda e, c=c, s2=s2: e.matmul(pb[5][:, 0:TT], lhsT=ONESB[:, :], rhs=sqv(s2), start=(c == 0), stop=(c == DC - 1)),
                         reads=[("sq", s2), "onesb"], writes=[("pb", 5)], signal=True)
                m1, m2, m3 = tmp_ring.next(), tmp_ring.next(), tmp_ring.next()
                P.op("dve", lambda e, m1=m1: e.tensor_scalar(out=tmpv(m1), in0=pb[5][:, 0:TT], scalar1=1.0 / D, scalar2=None, op0=ALU.mult),
                     reads=[("pb", 5)], writes=[("tmp", m1)])
                P.op("dve", lambda e, m1=m1, m2=m2: e.tensor_tensor(out=tmpv(m2), in0=tmpv(m1), in1=tmpv(m1), op=ALU.mult),
                     reads=[("tmp", m1)], writes=[("tmp", m2)])
                P.op("dve", lambda e, m2=m2, m3=m3: e.scalar_tensor_tensor(out=tmpv(m3), in0=pb[4][:, 0:TT], scalar=1.0 / D, in1=tmpv(m2),
                                                                          op0=ALU.mult, op1=ALU.subtract),
                     reads=[("pb", 4), ("tmp", m2)], writes=[("tmp", m3)])
                P.op("act", lambda e, m3=m3: e.activation(out=tmpv(m3), in_=tmpv(m3), func=AF.Sqrt, bias=EPSC[SUBLN_EPS], scale=1.0),
                     reads=[("tmp", m3), "epsc"], writes=[("tmp", m3)])
                P.op("dve", lambda e, m3=m3: e.reciprocal(out=tmpv(m3), in_=tmpv(m3)), reads=[("tmp", m3)], writes=[("tmp", m3)])
                for c in range(DC):
                    P.op("dve", lambda e, c=c, m1=m1: e.tensor_tensor(out=CT[:, c, :], in0=CT[:, c, :], in1=tmpv(m1), op=ALU.subtract),
                         reads=[("ct", c), ("tmp", m1)], writes=[("ct", c)])
                    P.op("dve", lambda e, c=c, m3=m3: e.scalar_tensor_tensor(out=CT[:, c, :], in0=CT[:, c, :], scalar=pc(f"lng{i}", c),
                                                                            in1=tmpv(m3), op0=ALU.mult, op1=ALU.mult),
                         reads=[("ct", c), ("tmp", m3), "par"], writes=[("ct", c)])
                    P.op("act", lambda e, c=c, tt=tt: e.activation(out=HB[:, c, tt * TT:(tt + 1) * TT], in_=CT[:, c, :], func=AF.Silu,
                                                                    bias=pc(f"lnb{i}", c)),
                         reads=[("ct", c), "par"], writes=[("h", c, tt)])
            for hf in range(2):
                P.barrier()
                proj_post(lambda kc, tt, hf=hf: (HB[:, kc, (hf * NTH + tt) * TT:(hf * NTH + tt + 1) * TT], ("h", kc, hf * NTH + tt)),
                          DC, pw2_d[ic], f"bpw2{i}", lambda c: pc(f"mpost{i}", c), hf)

        TABv = HS[:, SCR0:SCR0 + NTAB].bitcast(I16)
        KTv = HS[:, SCR0 + NTAB:SCR0 + NTAB + S]
        VHv = HS[:, SCR0 + NTAB + S:SCR0 + NTAB + 2 * S].rearrange("p (k e) -> p k e", e=128)
        QZv = HS[:, SCR0 + NTAB + 2 * S:SCR0 + NTAB + 2 * S + 2 * T].rearrange("p (m t) -> p m t", m=2)
        assert 2 * WSL >= 2 * S
        KTs = [KTv, WR[:, 0:S]]
        VHs = [VHv, WR[:, S:2 * S].rearrange("p (k e) -> p k e", e=128)]
        QZs = [QZv, FT[:, 12 * TT:12 * TT + T].bitcast(BF16).rearrange("p (m t) -> p m t", m=2)]
        SCv = lambda s: FT[:, s * TT:(s + 1) * TT]
        PTall = FT[:, 8 * TT:12 * TT].bitcast(BF16)
        PTv = lambda s: PTall[:, s * TT:(s + 1) * TT]
        STG = FT[:, 0:TT].bitcast(BF16)
        VW = min(512, D)
        STGV = FT[:, TT:TT + VW].bitcast(BF16)
        scale = 64 ** -0.5

        def attn(i, ia):
            P.barrier()
            li = lambda_init(i)
            dbase = 16 * i
            P.op("dve", lambda e: e.tensor_tensor(out=DER[:, dbase:dbase + 1], in0=pc(f"lq1{i}"), in1=pc(f"lk1{i}"), op=ALU.mult),
                 reads=["par"], writes=[("der", i)])
            P.op("dve", lambda e: e.tensor_tensor(out=DER[:, dbase + 1:dbase + 2], in0=pc(f"lq2{i}"), in1=pc(f"lk2{i}"), op=ALU.mult),
                 reads=["par"], writes=[("der", i)])
            P.op("pe", lambda e: e.matmul(pb[7][:, 0:2], lhsT=ONESF[:, :], rhs=DER[:, dbase:dbase + 2], start=True, stop=True),
                 reads=[("der", i), "onesf"], writes=[("pb", 7)])
            P.op("act", lambda e: e.activation(out=DER[:, dbase + 2:dbase + 4], in_=pb[7][:, 0:2], func=AF.Exp),
                 reads=[("pb", 7)], writes=[("der", i)])
            P.op("dve", lambda e: e.tensor_tensor(out=DER[:, dbase + 4:dbase + 5], in0=DER[:, dbase + 3:dbase + 4],
                                                  in1=DER[:, dbase + 2:dbase + 3], op=ALU.subtract),
                 reads=[("der", i)], writes=[("der", i)])
            P.op("dve", lambda e: e.tensor_scalar(out=DER[:, dbase + 5:dbase + 6], in0=DER[:, dbase + 4:dbase + 5], scalar1=-li, scalar2=None,
                                                  op0=ALU.add), reads=[("der", i)], writes=[("der", i)])
            P.op("dve", lambda e: e.tensor_scalar(out=DER[:, dbase + 6:dbase + 7], in0=pc(f"subg{i}"), scalar1=1.0 - li, scalar2=None,
                                                  op0=ALU.mult), reads=["par"], writes=[("der", i)])
            NEGLAM = DER[:, dbase + 5:dbase + 6]
            GSUB = DER[:, dbase + 6:dbase + 7]
            prenorm(f"mpre{i}", list(range(NT)), lambda c, n_: HB[:, c, n_ * TT:(n_ + 1) * TT])
            stg_ring, bank_ring = Ring(2), Ring(8)
            for off_w, dst, dres in ((0, q_loc, "q_loc"), (D, kt_loc, "kt_loc")):
                for c0 in range(0, D, GW):
                    w = min(GW, D - c0)
                    slot = wr_ring.next()
                    load_w(slot, 0, qkv_d[ia], off_w + c0, w)
                    for cl in range(w // 128):
                        c = c0 // 128 + cl
                        for tt in range(NT):
                            bk = bank_ring.next()
                            for kc in range(DC):
                                P.op("pe", lambda e, bk=bk, kc=kc, cl=cl, tt=tt, slot=slot, w=w: e.matmul(
                                    pb[bk][:, 0:TT], lhsT=wview(slot, 0, w)[:, kc, cl * 128:(cl + 1) * 128],
                                    rhs=HB[:, kc, tt * TT:(tt + 1) * TT], start=(kc == 0), stop=(kc == DC - 1)),
                                    reads=[("wr", slot, 0), ("h", kc, tt)], writes=[("pb", bk)], signal=(kc == DC - 1))
                            s = stg_ring.next()
                            eng = "act" if (tt % 2 == 0) else "dve"
                            if eng == "act":
                                P.op("act", lambda e, bk=bk, s=s: e.activation(out=STG[:, s * TT:(s + 1) * TT], in_=pb[bk][:, 0:TT], func=AF.Copy),
                                     reads=[("pb", bk)], writes=[("stg", s)])
                            else:
                                P.op("dve", lambda e, bk=bk, s=s: e.tensor_copy(out=STG[:, s * TT:(s + 1) * TT], in_=pb[bk][:, 0:TT]),
                                     reads=[("pb", bk)], writes=[("stg", s)])
                            P.dma("sp", lambda e, s=s, c=c, tt=tt, dst=dst: e.dma_start(
                                out=dst[c * 128:(c + 1) * 128, tt * TT:(tt + 1) * TT], in_=STG[:, s * TT:(s + 1) * TT]),
                                reads=[("stg", s)], writes=[dres])
            stgv_ring = Ring(2)
            for cg in range(D // VW):
                slot = wr_ring.next()
                load_w(slot, 0, qkv_d[ia], 2 * D + cg * VW, VW)
                for tk in range(T // 128):
                    bk = bank_ring.next()
                    for kc in range(DC):
                        P.op("pe", lambda e, bk=bk, kc=kc, tk=tk, slot=slot: e.matmul(
                            pb[bk][:, 0:VW], lhsT=HB[:, kc, tk * 128:(tk + 1) * 128], rhs=wview(slot, 0, VW)[:, kc, :],
                            start=(kc == 0), stop=(kc == DC - 1)),
                            reads=[("wr", slot, 0), ("h", kc, (tk * 128) // TT)], writes=[("pb", bk)], signal=(kc == DC - 1))
                    s = stgv_ring.next()
                    if tk % 2 == 0:
                        P.op("act", lambda e, bk=bk, s=s: e.activation(out=STGV[:, s * VW:(s + 1) * VW], in_=pb[bk][:, 0:VW], func=AF.Copy),
                             reads=[("pb", bk)], writes=[("stgv", s)])
                    else:
                        P.op("dve", lambda e, bk=bk, s=s: e.tensor_copy(out=STGV[:, s * VW:(s + 1) * VW], in_=pb[bk][:, 0:VW]),
                             reads=[("pb", bk)], writes=[("stgv", s)])
                    P.dma("sp", lambda e, s=s, tk=tk, cg=cg: e.dma_start(
                        out=v_loc[tk * 128:(tk + 1) * 128, cg * VW:(cg + 1) * VW], in_=STGV[:, s * VW:(s + 1) * VW]),
                        reads=[("stgv", s)], writes=["v_loc"])
            for h in range(H):
                P.collective(lambda e, h=h: e.collective_compute(
                    "AllGather", ALU.bypass, replica_groups=groups,
                    ins=[kt_loc[h * 128:(h + 1) * 128, :].opt()], outs=[kt_all[h * 256:(h + 1) * 256, :].opt()]),
                    reads=["kt_loc"], writes=["kt_all"])
            for j in range(NJ):
                P.collective(lambda e, j=j: e.collective_compute(
                    "AllGather", ALU.bypass, replica_groups=groups,
                    ins=[v_loc[j * CRV:(j + 1) * CRV, :].opt()], outs=[v_all[j * 2 * CRV:(j + 1) * 2 * CRV, :].opt()]),
                    reads=["v_loc"], writes=["v_all"])
            P.barrier()
            P.dma("sp", lambda e: e.dma_start(out=TABv, in_=tab_d[:, :]), writes=["tab"])
            for b_ in range(2):
                P.op("dve", lambda e, b_=b_: e.memset(QZs[b_][:, :, :], 0.0), writes=[("qz0", b_), ("qth", b_)])
            sc_ring, pt_ring, sb_ring = Ring(8), Ring(8), Ring(4)
            LOOK, LATE, pend, late = 6, min(12, (S // 128) // 2), [], []
            KPC = CRV // 128

            def head_loads(h):
                b_ = h % 2
                for r_ in range(2):
                    P.dma("sp", lambda e, r_=r_: e.dma_start(out=KTs[b_][:, r_ * T:(r_ + 1) * T],
                                                              in_=kt_all[h * 256 + r_ * 128: h * 256 + (r_ + 1) * 128, :]),
                          reads=["kt_all"], writes=[("kth", b_)])
                for m_ in range(2):
                    P.dma("sp", lambda e, m_=m_: e.dma_start(out=QZs[b_][m_ * 64:(m_ + 1) * 64, m_, :],
                                                              in_=q_loc[h * 128 + m_ * 64:h * 128 + (m_ + 1) * 64, :]),
                          reads=["q_loc", ("qz0", b_)], writes=[("qth", b_)])
                for r_ in range(2):
                    for j in range(NJ):
                        kt0 = r_ * (T // 128) + j * KPC
                        row0 = j * 2 * CRV + r_ * CRV
                        src = v_all[row0:row0 + CRV, h * 128:(h + 1) * 128].rearrange("(k p) e -> p k e", p=128)
                        P.dma("sp", lambda e, kt0=kt0, src=src: e.dma_start(out=VHs[b_][:, kt0:kt0 + KPC, :], in_=src),
                              reads=["v_all"], writes=[("vh", b_)])

            head_loads(0)
            for h in range(H):
                if h + 1 < H:
                    head_loads(h + 1)
                hb = h % 2
                KTc, VHc, QZc = KTs[hb], VHs[hb], QZs[hb]
                slope = 2.0 ** (-8.0 * (h + 1) / H)
                nk = S // 128
                for qt in range(NT):
                    osl = {}
                    for kt in range(nk):
                        for m in range(2):
                            ob, db = 2 * m, 2 * m + 1
                            sbk = 4 + sb_ring.next()
                            P.op("pe", lambda e, m=m, kt=kt, qt=qt, sbk=sbk, KTc=KTc, QZc=QZc: e.matmul(
                                pb[sbk][:, 0:TT], lhsT=KTc[:, kt * 128:(kt + 1) * 128],
                                rhs=QZc[:, m, qt * TT:(qt + 1) * TT], start=True, stop=True),
                                reads=[("kth", hb), ("qth", hb)], writes=[("pb", sbk)])
                            s1 = sc_ring.next()
                            toff = qt * TT - kt * 128 + (S - 128)
                            P.op("dve", lambda e, s1=s1, sbk=sbk, toff=toff, slope=slope: e.scalar_tensor_tensor(
                                out=SCv(s1), in0=TABv[:, toff:toff + TT], scalar=-slope / scale, in1=pb[sbk][:, 0:TT],
                                op0=ALU.mult, op1=ALU.add), reads=["tab", ("pb", sbk)], writes=[("sc", s1)])
                            s2 = pt_ring.next()
                            P.op("act", lambda e, s1=s1, s2=s2: e.activation(out=PTv(s2), in_=SCv(s1), func=AF.Exp, scale=scale),
                                 reads=[("sc", s1)], writes=[("pt", s2)])

                            def pv(s2=s2, kt=kt, ob=ob, db=db, m=m, qt=qt, osl=osl, h=h, VHc=VHc, hb=hb):
                                P.op("pe", lambda e: e.matmul(
                                    pb[ob][:, 0:TT], lhsT=VHc[:, kt, :], rhs=PTv(s2), start=(kt == 0), stop=(kt == nk - 1)),
                                    reads=[("vh", hb), ("pt", s2)], writes=[("pb", ob)], signal=False)
                                P.op("pe", lambda e: e.matmul(
                                    pb[db][:, 0:TT], lhsT=ONESB[:, :], rhs=PTv(s2), start=(kt == 0), stop=(kt == nk - 1)),
                                    reads=["onesb", ("pt", s2)], writes=[("pb", db), ("pb", ob)], signal=True)
                                if kt != nk - 1:
                                    return
                                rc, om = tmp_ring.next(), tmp_ring.next()
                                P.op("act", lambda e: e.activation(out=tmpv(rc), in_=pb[db][:, 0:TT], func=AF.Ln),
                                     reads=[("pb", db)], writes=[("tmp", rc)])
                                P.op("act", lambda e: e.activation(out=tmpv(rc), in_=tmpv(rc), func=AF.Exp, scale=-1.0),
                                     reads=[("tmp", rc)], writes=[("tmp", rc)])
                                P.op("dve", lambda e: e.tensor_tensor(out=tmpv(om), in0=pb[ob][:, 0:TT], in1=tmpv(rc), op=ALU.mult),
                                     reads=[("pb", ob), ("tmp", rc)], writes=[("tmp", om)])
                                osl[m] = om
                                if m == 0:
                                    return

                                def tail(o0=osl[0], o1=osl[1]):
                                    oh = tmp_ring.next()
                                    P.op("dve", lambda e: e.scalar_tensor_tensor(
                                        out=tmpv(oh), in0=tmpv(o1), scalar=NEGLAM, in1=tmpv(o0), op0=ALU.mult, op1=ALU.add),
                                        reads=[("tmp", o0), ("tmp", o1), ("der", i)], writes=[("tmp", oh)])
                                    r = rms_rstd(lambda c: (tmpv(oh), ("tmp", oh)), 1, SUBLN_EPS, 1.0 / 128, 4 + sb_ring.next(), use_ln=True)
                                    P.op("dve", lambda e: e.scalar_tensor_tensor(
                                        out=HB[:, h, qt * TT:(qt + 1) * TT], in0=tmpv(oh), scalar=GSUB, in1=tmpv(r), op0=ALU.mult, op1=ALU.mult),
                                        reads=[("tmp", oh), ("tmp", r), ("der", i)], writes=[("h", h, qt)])
                                late.append([LATE, tail])

                            pend.append(pv)
                            while len(pend) > LOOK:
                                pend.pop(0)()
                            for it in list(late):
                                it[0] -= 1
                                if it[0] <= 0:
                                    late.remove(it)
                                    it[1]()
                while pend:
                    pend.pop(0)()
                while late:
                    late.pop(0)[1]()
            for hf in range(2):
                P.barrier()
                proj_post(lambda kc, tt, hf=hf: (HB[:, kc, (hf * NTH + tt) * TT:(hf * NTH + tt + 1) * TT], ("h", kc, hf * NTH + tt)),
                          DC, wo_d[ia], None, lambda c: pc(f"mpost{i}", c), hf)

        ic = ia = 0
        for i in range(L):
            ffn(i, 0)
            if i % 2 == 0:
                conv(i, ic)
                ic += 1
            else:
                attn(i, ia)
                ia += 1
            ffn(i, 1)
        P.barrier()
        for c in range(DC):
            P.dma("sp", lambda e, c=c: e.dma_start(out=yT_d[c * 128:(c + 1) * 128, :], in_=Xv[:, c, :]),
                  reads=[("x", c, t) for t in range(NT)], writes=[("y", c)])
        P.barrier()

        @block.tensor
        def _(e):
            P.replay("pe", e, sems)

        @block.scalar
        def _(e):
            P.replay("act", e, sems)

        @block.vector
        def _(e):
            P.replay("dve", e, sems)

        @block.gpsimd
        def _(e):
            P.replay("pool", e, sems)

        @block.sync
        def _(e):
            P.replay("sp", e, sems)
    return nc


def pack_inputs(cfg, inp, core):
    D, T, DC, L, S = cfg.D, cfg.T, cfg.DC, cfg.L, cfg.S
    poff, NP = param_layout(cfg)
    b, r = core // 2, core % 2
    par = np.zeros((128, NP), np.float32)
    par[:, poff["ident"]:poff["ident"] + 128] = np.eye(128, dtype=np.float32)
    par[:, poff["mL"]] = 1.0 if r == 1 else 0.0
    par[:, poff["mR"]] = 1.0 if r == 0 else 0.0

    def vec(name, v):
        v = np.asarray(v, np.float32)
        n = v.shape[0] // 128
        par[:, poff[name]:poff[name] + n] = v.reshape(n, 128).T

    ic = ia = 0
    for i in range(L):
        for j in range(2):
            vec(f"fpre{i}{j}", inp["ffn_norm_pre"][i, j])
            vec(f"fpost{i}{j}", inp["ffn_norm_post"][i, j])
        vec(f"mpre{i}", inp["mix_norm_pre"][i])
        vec(f"mpost{i}", inp["mix_norm_post"][i])
        if i % 2 == 0:
            vec(f"bpw1{i}", inp["conv_b_pw1"][ic])
            wdw = np.asarray(inp["conv_w_dw"][ic], np.float32)
            o = poff[f"wdw{i}"]
            par[:, o:o + DC * CONVW] = wdw.T.reshape(DC, 128, CONVW).transpose(1, 0, 2).reshape(128, DC * CONVW)
            vec(f"bdw{i}", inp["conv_b_dw"][ic])
            vec(f"lng{i}", inp["conv_ln_g"][ic])
            vec(f"lnb{i}", inp["conv_ln_b"][ic])
            vec(f"bpw2{i}", inp["conv_b_pw2"][ic])
            ic += 1
        else:
            par[:, poff[f"subg{i}"]] = np.asarray(inp["attn_subln_g"][ia], np.float32)
            for nm, key in (("lq1", "attn_lambda_q1"), ("lk1", "attn_lambda_k1"), ("lq2", "attn_lambda_q2"), ("lk2", "attn_lambda_k2")):
                par[0:64, poff[f"{nm}{i}"]] = np.asarray(inp[key][ia], np.float32)
            ia += 1
    m = np.arange(cfg.NTAB, dtype=np.int64)[None, :]
    p = np.arange(128, dtype=np.int64)[:, None]
    tab = np.abs(m - (S - 128) + r * T - p).astype(np.int16)
    xT = np.ascontiguousarray(np.asarray(inp["x"], np.float32)[b, r * T:(r + 1) * T, :].T)
    f32 = lambda a: np.ascontiguousarray(np.asarray(a, np.float32))
    return {"xT": xT, "par": par, "tab": tab, "wg": f32(inp["ffn_w_gate"]), "wu": f32(inp["ffn_w_up"]), "wd": f32(inp["ffn_w_down"]),
            "pw1": f32(inp["conv_w_pw1"]), "pw2": f32(inp["conv_w_pw2"]), "qkv": f32(inp["attn_w_qkv"]), "wo": f32(inp["attn_w_o"])}


def kernel(**inputs):
    cfg = Cfg()
    nc = build(cfg)
    in_maps = [pack_inputs(cfg, inputs, c) for c in range(cfg.ncores)]
    res = run_bass_kernel_spmd(nc, in_maps, core_ids=list(range(cfg.ncores)))
    B = cfg.ncores // 2
    out = np.empty((B, cfg.S, cfg.D), np.float32)
    for c in range(cfg.ncores):
        out[c // 2, (c % 2) * cfg.T:(c % 2 + 1) * cfg.T, :] = np.asarray(res.results[c]["yT"]).T
    return out
```
